# Optimizing a Trainium2 kernel written in Bass

```python
import math
import jax, jax.numpy as jnp
from jax import lax
import numpy as np

D_MODEL = 1024
BATCH = 8
SEQ = 2048
DEPTH = 1

CHUNK = 64
Q_BLOCK = 128
HEAD_DIM = 64
N_HEADS_DIFF = D_MODEL // 256
DIFF_V_DIM = 2 * HEAD_DIM
N_HEADS_CHUNK = D_MODEL // 128
LEFT_CHUNKS = 8
BAND = (LEFT_CHUNKS + 1) * CHUNK
REL_CLIP = 128
N_MEM = 256
N_HEADS_MEM = 4
MEM_HEAD_DIM = D_MODEL // N_HEADS_MEM
D_FF = 4 * D_MODEL
ROPE_THETA = 10000.0
LN_EPS = 1e-5
NEG_INF = -1e30
DEEPNORM_ALPHA = (2.0 * DEPTH) ** 0.25
DEEPNORM_BETA = (8.0 * DEPTH) ** -0.25

W_DIFF_QK = N_HEADS_DIFF * 2 * HEAD_DIM
W_DIFF_V = N_HEADS_DIFF * DIFF_V_DIM
W_CHUNK = N_HEADS_CHUNK * HEAD_DIM
MIX_WIDTH = W_DIFF_V + W_CHUNK
IN_SPLITS = [W_DIFF_QK, W_DIFF_QK, W_DIFF_V, W_CHUNK, W_CHUNK, W_CHUNK]
IN_WIDTH = sum(IN_SPLITS)

kernel_name = "hybrid_diffattn_chunkrel_stream_layer"


def layer_norm(x, g, b):
    xf = x.astype(jnp.float32)
    mu = jnp.mean(xf, axis=-1, keepdims=True)
    var = jnp.mean(jnp.square(xf - mu), axis=-1, keepdims=True)
    y = (xf - mu) * lax.rsqrt(var + LN_EPS) * g.astype(jnp.float32) + b.astype(jnp.float32)
    return y.astype(x.dtype)


def rms_norm(x, g):
    xf = x.astype(jnp.float32)
    y = xf * lax.rsqrt(jnp.mean(jnp.square(xf), axis=-1, keepdims=True) + LN_EPS)
    return (y * g.astype(jnp.float32)).astype(x.dtype)


def rope(x, positions):
    d = x.shape[-1]
    inv_freq = 1.0 / (ROPE_THETA ** (jnp.arange(0, d, 2, dtype=jnp.float32) / d))
    ang = positions.astype(jnp.float32)[..., None] * inv_freq
    ang = ang.reshape(ang.shape[:2] + (1,) * (x.ndim - 3) + ang.shape[-1:])
    cos, sin = jnp.cos(ang), jnp.sin(ang)
    xf = x.astype(jnp.float32)
    x1, x2 = xf[..., : d // 2], xf[..., d // 2:]
    return jnp.concatenate([x1 * cos - x2 * sin, x2 * cos + x1 * sin], axis=-1).astype(x.dtype)


def diff_attention(q, k, v, lam_vecs, lambda_init, subln_g):
    B, S, H, _, d = q.shape
    nb = S // Q_BLOCK
    lam = (jnp.exp(jnp.sum(lam_vecs[0].astype(jnp.float32) * lam_vecs[1].astype(jnp.float32)))
           - jnp.exp(jnp.sum(lam_vecs[2].astype(jnp.float32) * lam_vecs[3].astype(jnp.float32)))
           + lambda_init)
    q = q * (d ** -0.5)
    q_blocks = jnp.moveaxis(q.reshape(B, nb, Q_BLOCK, H, 2, d), 1, 0)
    q_chunk = (jnp.arange(S) // CHUNK).reshape(nb, Q_BLOCK)
    k_chunk = jnp.arange(S) // CHUNK

    def one_block(args):
        qb, qc = args
        s = jnp.einsum('bqhtd,bkhtd->bthqk', qb, k).astype(jnp.float32)
        allowed = k_chunk[None, :] <= qc[:, None]
        p = jax.nn.softmax(jnp.where(allowed, s, NEG_INF), axis=-1)
        a = p[:, 0] - lam * p[:, 1]
        return jnp.einsum('bhqk,bkhe->bqhe', a.astype(v.dtype), v)

    o = lax.map(one_block, (q_blocks, q_chunk))
    o = jnp.moveaxis(o, 0, 1).reshape(B, S, H, v.shape[-1])
    return rms_norm(o, subln_g) * (1.0 - lambda_init)


def band_chunks(t, nc):
    B, S, H, d = t.shape
    tc = t.reshape(B, nc, CHUNK, H, d)
    tp = jnp.pad(tc, ((0, 0), (LEFT_CHUNKS, 0), (0, 0), (0, 0), (0, 0)))
    return jnp.concatenate([tp[:, j:j + nc] for j in range(LEFT_CHUNKS + 1)], axis=2)


def chunk_rel_attention(q, k, v, rel_bias):
    B, S, H, d = q.shape
    nc = S // CHUNK
    qc = (q * (d ** -0.5)).reshape(B, nc, CHUNK, H, d)
    kband = band_chunks(k, nc)
    vband = band_chunks(v, nc)
    s = jnp.einsum('bnqhd,bnkhd->bnhqk', qc, kband).astype(jnp.float32)
    qi = np.arange(CHUNK)[:, None]
    kj = np.arange(BAND)[None, :]
    rel_idx = np.clip(qi + LEFT_CHUNKS * CHUNK - kj, -REL_CLIP, REL_CLIP) + REL_CLIP
    bias = rel_bias[:, rel_idx].astype(jnp.float32)
    key_pos = (jnp.arange(nc)[:, None] - LEFT_CHUNKS) * CHUNK + jnp.arange(BAND)[None, :]
    valid = key_pos >= 0
    s = jnp.where(valid[None, :, None, None, :], s + bias[None, None], NEG_INF)
    p = jax.nn.softmax(s, axis=-1)
    o = jnp.einsum('bnhqk,bnkhd->bnqhd', p.astype(v.dtype), vband)
    return o.reshape(B, S, H, d)


def hybrid_mixer(x, positions, w_in, lam_vecs, lambda_init, subln_g, rel_bias, w_o):
    B, S, _ = x.shape
    h = x @ w_in
    qa, ka, va, qb, kb, vb = jnp.split(h, np.cumsum(IN_SPLITS)[:-1], axis=-1)
    qa = rope(qa.reshape(B, S, N_HEADS_DIFF, 2, HEAD_DIM), positions)
    ka = rope(ka.reshape(B, S, N_HEADS_DIFF, 2, HEAD_DIM), positions)
    va = va.reshape(B, S, N_HEADS_DIFF, DIFF_V_DIM)
    ya = diff_attention(qa, ka, va, lam_vecs, lambda_init, subln_g)
    yb = chunk_rel_attention(qb.reshape(B, S, N_HEADS_CHUNK, HEAD_DIM),
                             kb.reshape(B, S, N_HEADS_CHUNK, HEAD_DIM),
                             vb.reshape(B, S, N_HEADS_CHUNK, HEAD_DIM), rel_bias)
    y = jnp.concatenate([ya.reshape(B, S, W_DIFF_V), yb.reshape(B, S, W_CHUNK)], axis=-1)
    return y @ w_o


def memory_cross_attention(x, mem, w_mq, w_mk, w_mv, w_mo):
    B, S, _ = x.shape
    M = mem.shape[1]
    q = (x @ w_mq).reshape(B, S, N_HEADS_MEM, MEM_HEAD_DIM) * (MEM_HEAD_DIM ** -0.5)
    k = (mem @ w_mk).reshape(B, M, N_HEADS_MEM, MEM_HEAD_DIM)
    v = (mem @ w_mv).reshape(B, M, N_HEADS_MEM, MEM_HEAD_DIM)
    s = jnp.einsum('bshd,bmhd->bhsm', q, k).astype(jnp.float32)
    p = jax.nn.softmax(s, axis=-1)
    o = jnp.einsum('bhsm,bmhd->bshd', p.astype(v.dtype), v).reshape(B, S, D_MODEL)
    return o @ w_mo


def sq_relu_mlp(x, w_up, w_down):
    return jnp.square(jax.nn.relu(x @ w_up)) @ w_down


def setup_inputs(seed: int = 0) -> dict:
    key = jax.random.key(seed)
    ks = jax.random.split(key, 24)
    f32 = jnp.float32
    beta = DEEPNORM_BETA
    x = jax.random.normal(ks[0], (BATCH, SEQ, D_MODEL), f32)
    mem = jax.random.normal(ks[1], (BATCH, N_MEM, D_MODEL), f32)
    start = jax.random.randint(ks[2], (BATCH, 1), 0, 64, dtype=jnp.int32) * CHUNK
    positions = (start + jnp.arange(SEQ, dtype=jnp.int32)[None, :]).astype(jnp.int32)
    col_scale = jnp.concatenate([
        jnp.ones((2 * W_DIFF_QK,), f32), jnp.full((W_DIFF_V,), beta, f32),
        jnp.ones((2 * W_CHUNK,), f32), jnp.full((W_CHUNK,), beta, f32)])
    w_in = jax.random.normal(ks[3], (DEPTH, D_MODEL, IN_WIDTH), f32) * (D_MODEL ** -0.5) * col_scale
    diff_lambda = jax.random.normal(ks[4], (DEPTH, 4, HEAD_DIM), f32) * 0.1
    subln_g = 1.0 + 0.02 * jax.random.normal(ks[5], (DEPTH, DIFF_V_DIM), f32)
    rel_bias = 0.2 * jax.random.normal(ks[6], (DEPTH, N_HEADS_CHUNK, 2 * REL_CLIP + 1), f32)
    w_o = jax.random.normal(ks[7], (DEPTH, MIX_WIDTH, D_MODEL), f32) * (MIX_WIDTH ** -0.5) * beta
    ln1_g = 1.0 + 0.02 * jax.random.normal(ks[8], (DEPTH, D_MODEL), f32)
    ln1_b = 0.02 * jax.random.normal(ks[9], (DEPTH, D_MODEL), f32)
    w_mq = jax.random.normal(ks[10], (DEPTH, D_MODEL, D_MODEL), f32) * (D_MODEL ** -0.5)
    w_mk = jax.random.normal(ks[11], (DEPTH, D_MODEL, D_MODEL), f32) * (D_MODEL ** -0.5)
    w_mv = jax.random.normal(ks[12], (DEPTH, D_MODEL, D_MODEL), f32) * (D_MODEL ** -0.5) * beta
    w_mo = jax.random.normal(ks[13], (DEPTH, D_MODEL, D_MODEL), f32) * (D_MODEL ** -0.5) * beta
    ln2_g = 1.0 + 0.02 * jax.random.normal(ks[14], (DEPTH, D_MODEL), f32)
    ln2_b = 0.02 * jax.random.normal(ks[15], (DEPTH, D_MODEL), f32)
    w_up = jax.random.normal(ks[16], (DEPTH, D_MODEL, D_FF), f32) * (D_MODEL ** -0.5) * beta
    w_down = jax.random.normal(ks[17], (DEPTH, D_FF, D_MODEL), f32) * (D_FF ** -0.5) * beta
    ln3_g = 1.0 + 0.02 * jax.random.normal(ks[18], (DEPTH, D_MODEL), f32)
    ln3_b = 0.02 * jax.random.normal(ks[19], (DEPTH, D_MODEL), f32)
    return {"x": x, "mem": mem, "positions": positions, "w_in": w_in,
            "diff_lambda": diff_lambda, "subln_g": subln_g, "rel_bias": rel_bias,
            "w_o": w_o, "ln1_g": ln1_g, "ln1_b": ln1_b,
            "w_mq": w_mq, "w_mk": w_mk, "w_mv": w_mv, "w_mo": w_mo,
            "ln2_g": ln2_g, "ln2_b": ln2_b, "w_up": w_up, "w_down": w_down,
            "ln3_g": ln3_g, "ln3_b": ln3_b}


def reference(x, mem, positions, w_in, diff_lambda, subln_g, rel_bias, w_o, ln1_g, ln1_b,
              w_mq, w_mk, w_mv, w_mo, ln2_g, ln2_b, w_up, w_down, ln3_g, ln3_b):
    alpha = DEEPNORM_ALPHA
    for l in range(DEPTH):
        lambda_init = 0.8 - 0.6 * math.exp(-0.3 * l)
        y = hybrid_mixer(x, positions, w_in[l], diff_lambda[l], lambda_init, subln_g[l],
                         rel_bias[l], w_o[l])
        x = layer_norm(alpha * x + y, ln1_g[l], ln1_b[l])
        y = memory_cross_attention(x, mem, w_mq[l], w_mk[l], w_mv[l], w_mo[l])
        x = layer_norm(alpha * x + y, ln2_g[l], ln2_b[l])
        y = sq_relu_mlp(x, w_up[l], w_down[l])
        x = layer_norm(alpha * x + y, ln3_g[l], ln3_b[l])
    return x
```

```python
import math
from contextlib import ExitStack

import numpy as np
import ml_dtypes

import concourse.bass as bass
import concourse.mybir as mybir
from concourse.bass_utils import run_bass_kernel_spmd

F32 = mybir.dt.float32
BF16 = mybir.dt.bfloat16
I32 = mybir.dt.int32
AF = mybir.ActivationFunctionType
ALU = mybir.AluOpType

S_LEN = 2048
D = 1024
NT = 16
NG = 4
KC = 8
ALPHA = 2.0 ** 0.25
LAMBDA_INIT = 0.8 - 0.6 * math.exp(0.0)
LN_EPS = 1e-5
MASK_NEG = -1.0e4
CW1 = 6.28125
CW2 = 2.0 * math.pi - 6.28125
ENGS = ("pe", "act", "dve", "pool", "sp")


class Res:
    __slots__ = ("name", "w", "r")

    def __init__(self, name=""):
        self.name = name
        self.w = None
        self.r = {}


class Op:
    __slots__ = ("eng", "fn", "deps", "dma", "sig", "sem", "val", "slot", "gidx")

    def __init__(self, eng, fn, dma):
        self.eng = eng
        self.fn = fn
        self.dma = dma
        self.deps = ()
        self.sig = False
        self.sem = None
        self.val = 0
        self.slot = -1


class Sched:
    def __init__(self, nc, n_dma_sems=None):
        self.nc = nc
        self.ops = []
        self.eng_ops = {e: [] for e in ENGS}
        self.n_dma_sems = n_dma_sems or {"sp": 32, "pool": 12, "act": 2}
        self.dma_base = {}
        base = 0
        for q_, n_ in self.n_dma_sems.items():
            self.dma_base[q_] = base
            base += n_
        self.n_dma_total = base
        self.dma_count = {q_: 0 for q_ in self.n_dma_sems}
        self.dma_hist = {}

    def _add(self, eng, fn, reads, writes, dma):
        o = Op(eng, fn, dma)
        deps = set()
        for r in reads:
            if r.w is not None:
                deps.add(r.w)
        for w in writes:
            if w.w is not None:
                deps.add(w.w)
            deps.update(w.r.values())
        if eng == "pe" and not dma:
            deps = {d for d in deps if d.dma or d.eng != "pe"}
        if dma:
            slot = self.dma_base[eng] + self.dma_count[eng] % self.n_dma_sems[eng]
            self.dma_count[eng] += 1
            o.slot = slot
            prev = self.dma_hist.get(slot)
            if prev is not None:
                deps.add(prev)
                o.val = prev.val + 16
            else:
                o.val = 16
            self.dma_hist[slot] = o
            o.sig = True
        o.deps = deps
        for d in deps:
            d.sig = True
        key = o if dma else eng
        for r in reads:
            r.r[key] = o
        for w in writes:
            w.w = o
            w.r = {}
        o.gidx = len(self.ops)
        self.ops.append(o)
        self.eng_ops[eng].append(o)
        return o

    def op(self, eng, fn, reads=(), writes=()):
        return self._add(eng, fn, reads, writes, False)

    def dma(self, queue, fn, reads=(), writes=()):
        return self._add(queue, fn, reads, writes, True)

    def fence(self, eng, after):
        o = self._add(eng, lambda e: e.nop(), (), (), False)
        for d in after:
            d.sig = True
        o.deps = set(o.deps) | set(after)
        return o

    def emit(self, stack):
        nc = self.nc
        esem = {e: stack.enter_context(nc.semaphore(f"s_{e}")) for e in ENGS}
        dsem = [stack.enter_context(nc.semaphore(f"s_dma{i}")) for i in range(self.n_dma_total)]
        for e in ENGS:
            c = 0
            for o in self.eng_ops[e]:
                if o.dma:
                    o.sem = dsem[o.slot]
                elif o.sig:
                    c += 1
                    o.sem = esem[e]
                    o.val = c
        know = {}
        cur = {e: {} for e in ENGS}
        waits = {}
        for o in self.ops:
            k = cur[o.eng]
            wl = []
            need = {}
            for d in o.deps:
                nm = d.sem.name
                if need.get(nm, (0, None))[0] < d.val:
                    need[nm] = (d.val, d)
            for nm, (v, d) in sorted(need.items(), key=lambda kv: kv[1][1].gidx):
                if k.get(nm, 0) >= v:
                    continue
                wl.append((d.sem, v))
                dk = know.get(d)
                if dk:
                    for a, b in dk.items():
                        if k.get(a, 0) < b:
                            k[a] = b
                if k.get(nm, 0) < v:
                    k[nm] = v
            waits[o] = wl
            if o.sig:
                snap = dict(k)
                snap[o.sem.name] = max(snap.get(o.sem.name, 0), o.val)
                know[o] = snap
        block = stack.enter_context(nc.Block())
        engmap = {"pe": block.tensor, "act": block.scalar, "dve": block.vector,
                  "pool": block.gpsimd, "sp": block.sync}
        for e in ENGS:
            ops = self.eng_ops[e]

            def body(eng, ops=ops):
                for o in ops:
                    for (s, v) in waits[o]:
                        eng.wait_ge(s, v)
                    ins = o.fn(eng)
                    if o.sig:
                        ins.then_inc(o.sem, 16 if o.dma else 1)
            engmap[e](body)


class Arena:
    def __init__(self, ap, nbytes):
        self.ap = ap
        self.nbytes = nbytes
        self.top = 0
        self.live = []
        self.ghosts = []

    def mark(self):
        return (self.top, len(self.live))

    def release(self, m):
        top, n = m
        for (a, b, rl) in self.live[n:]:
            self.ghosts.append((a, b, rl))
        del self.live[n:]
        self.top = top

    def alloc(self, nbytes, nres=1, name="", fine=True):
        a = (self.top + 31) // 32 * 32
        b = a + nbytes
        assert b <= self.nbytes, f"arena overflow {name}: {b} > {self.nbytes}"
        self.top = b
        inherit = {}
        keep = []
        for (ga, gb, rl) in self.ghosts:
            if ga < b and a < gb:
                for r in rl:
                    if r.w is not None:
                        inherit[("g", id(r), "w")] = r.w
                    for kk, vv in r.r.items():
                        inherit[("g", id(r), kk)] = vv
            keep.append((ga, gb, rl))
        res = []
        for i in range(nres):
            r = Res(f"{name}{i}")
            r.r = dict(inherit)
            res.append(r)
        if fine and nres > 1 and nbytes % nres == 0:
            sz = nbytes // nres
            for i, r in enumerate(res):
                lo, hi = a + i * sz, a + (i + 1) * sz
                inh = {}
                for (ga, gb, rl) in self.ghosts:
                    if ga < hi and lo < gb:
                        for g_ in rl:
                            if g_.w is not None:
                                inh[("g", id(g_), "w")] = g_.w
                            for kk, vv in g_.r.items():
                                inh[("g", id(g_), kk)] = vv
                r.r = inh
                self.live.append((lo, hi, [r]))
        else:
            self.live.append((a, b, res))
        return a, res

    def f32(self, n, nres=1, name=""):
        a, res = self.alloc(4 * n, nres, name)
        return self.ap[:, a // 4: a // 4 + n], res

    def bf16(self, n, nres=1, name="", fine=True):
        a, res = self.alloc(2 * n, nres, name, fine)
        return self.ap[:, a // 4: a // 4 + n // 2].bitcast(BF16), res

    def i32(self, n, nres=1, name=""):
        a, res = self.alloc(4 * n, nres, name)
        return self.ap[:, a // 4: a // 4 + n].bitcast(I32), res


ARENA_BYTES = 210944


def build_program(stop_after=99, debug=()):
    nc = bass.Bass("TRN2", target_bir_lowering=False)

    def din(name, shape, dt=F32):
        return nc.dram_tensor(name, list(shape), dt, kind="ExternalInput").ap()

    x_d = din("x", [S_LEN, D])
    mem_d = din("mem", [256, D])
    pos_d = din("posb", [128, S_LEN], I32)
    w_in_d = din("w_in", [D, 3072])
    w_o_d = din("w_o", [D, D])
    w_mq_d = din("w_mq", [D, D])
    w_mk_d = din("w_mk", [D, D])
    w_mv_d = din("w_mv", [D, D])
    w_mo_d = din("w_mo", [D, D])
    w_up_d = din("w_up", [D, 4096])
    w_dn_d = din("w_down", [4096, D])
    lam_d = din("lamb", [128, 256])
    subg_d = din("subg", [128, 1])
    bt_d = din("bt", [8, 128, 256])
    btc_d = din("btc", [128, 16])
    lnp_d = din("lnp", [3, 2, 128, D])
    cf_d = din("cf32", [128, 8])
    cb_d = din("cbf", [128, 256], BF16)
    out_d = nc.dram_tensor("out", [S_LEN, D], F32, kind="ExternalOutput").ap()

    with ExitStack() as st:
        AR = st.enter_context(nc.sbuf_tensor("arena", [128, ARENA_BYTES // 4], F32))
        PSALL = st.enter_context(nc.psum_tensor("psall", [128, 8 * 512], F32))
        PS = [PSALL[:, i * 512:(i + 1) * 512] for i in range(8)]

        def pspair(b0):
            return PSALL[:, b0 * 512:(b0 + 2) * 512].rearrange("p (t n) -> p t n", t=2)
        RPS = [Res(f"ps{i}") for i in range(8)]
        S = Sched(nc)
        A = Arena(AR, ARENA_BYTES)
        out_dmas = []

        def mm(out, lhsT, rhs, start, stop, reads, writes, skip=False):
            if skip:
                S.op("pe", lambda e: e.matmul(out, lhsT, rhs, start=start, stop=stop, skip_group_check=True),
                     reads=reads, writes=writes)
            else:
                S.op("pe", lambda e: e.matmul(out, lhsT, rhs, start=start, stop=stop), reads=reads, writes=writes)

        def tr(out, in_, reads, writes):
            S.op("pe", lambda e: e.transpose(out, in_, ident), reads=reads, writes=writes)

        def act(out, in_, func, reads, writes, scale=1.0, bias=None):
            if bias is None:
                S.op("act", lambda e: e.activation(out, in_, func, scale=scale), reads=reads, writes=writes)
            else:
                S.op("act", lambda e: e.activation(out, in_, func, bias=bias, scale=scale),
                     reads=reads, writes=writes)

        def acopy(out, in_, reads, writes):
            S.op("act", lambda e: e.copy(out, in_), reads=reads, writes=writes)

        def vcopy(out, in_, reads, writes):
            S.op("dve", lambda e: e.tensor_copy(out, in_), reads=reads, writes=writes)

        def tt(out, in0, in1, op, reads, writes):
            S.op("dve", lambda e: e.tensor_tensor(out, in0, in1, op), reads=reads, writes=writes)

        def ts(out, in0, s1, s2, op0, op1, reads, writes):
            if s2 is None:
                S.op("dve", lambda e: e.tensor_scalar(out, in0, s1, None, op0), reads=reads, writes=writes)
            else:
                S.op("dve", lambda e: e.tensor_scalar(out, in0, s1, s2, op0, op1), reads=reads, writes=writes)

        def stt(out, in0, scalar, in1, op0, op1, reads, writes):
            S.op("dve", lambda e: e.scalar_tensor_tensor(out, in0, scalar, in1, op0, op1),
                 reads=reads, writes=writes)

        def recip(out, in_, reads, writes):
            S.op("dve", lambda e: e.reciprocal(out, in_), reads=reads, writes=writes)

        def vmemset(out, val, writes):
            S.op("dve", lambda e: e.memset(out, val), writes=writes)

        def dma(q, out, in_, reads=(), writes=()):
            return S.dma(q, lambda e: e.dma_start(out=out, in_=in_), reads=reads, writes=writes)

        def dump(name, ap_sb, shape, dt, reads):
            if name not in debug:
                return
            d = nc.dram_tensor(name, list(shape), dt, kind="ExternalOutput").ap()
            out_dmas.append(dma("sp", d, ap_sb, reads=reads))

        cf, r_cf = A.f32(8, 1, "cf")
        cb, r_cb = A.bf16(256, 1, "cb")
        ones_bf, r_ob = A.bf16(128, 1, "onesb")
        ones_f, r_of = A.f32(128, 1, "onesf")
        smallf, r_small = A.f32(16, 1, "small")
        actt_flat, r_actt = A.bf16(KC * S_LEN, NT, "actt")
        ACTT = actt_flat.rearrange("p (c t) -> p c t", c=KC)
        ring = []
        for i in range(3):
            wv_, rw_ = A.bf16(4096, 1, f"ring{i}")
            ring.append((wv_, rw_[0]))
        ident = cb[:, 0:128]
        perm = cb[:, 128:256]
        invf = cf[:, 0:1]
        sgn = cf[:, 1:2]
        epsc = cf[:, 2:3]
        maskb = cf[:, 3:4]
        maskb2 = cf[:, 4:5]
        invf_lo = cf[:, 5:6]
        R_CF, R_CB, R_OB, R_OF, R_SM = r_cf[0], r_cb[0], r_ob[0], r_of[0], r_small[0]

        dma("sp", cf, cf_d, writes=[R_CF])
        dma("sp", cb, cb_d, writes=[R_CB])
        vmemset(ones_bf, 1.0, [R_OB])
        vmemset(ones_f, 1.0, [R_OF])

        pieces = []
        for off in (1024, 2560, 1536, 2048, 0, 512):
            pieces.append(("c", w_in_d, off))
        for (w_, off_) in ((w_o_d, 0), (w_o_d, 512), (w_mq_d, 0), (w_mk_d, 0), (w_mk_d, 512),
                           (w_mv_d, 0), (w_mv_d, 512), (w_mq_d, 512), (w_mo_d, 0), (w_mo_d, 512)):
            pieces.append(("c", w_, off_))
        for fg_ in range(8):
            pieces.append(("c", w_up_d, fg_ * 512))
            pieces.append(("r", w_dn_d, fg_ * 512))
        issued = [None] * len(pieces)
        piece_state = {"next": 0}

        def issue_piece(k):
            kind, w_d, off = pieces[k]
            wv_, rw_ = ring[3 if k == len(pieces) - 1 else k % 3]
            if kind == "c":
                v = wv_.rearrange("p (k n) -> p k n", k=8)
                src = w_d[:, off:off + 512].rearrange("(k p) n -> p k n", p=128)
            else:
                v = wv_.rearrange("p (k n) -> p k n", k=4)
                src = w_d[off:off + 512, :].rearrange("(k p) n -> p k n", p=128)
            dma("pool", v, src, writes=[rw_])
            issued[k] = (v, rw_)

        def next_piece(w_d, off, ahead=2):
            k = piece_state["next"]
            piece_state["next"] += 1
            assert pieces[k][1] is w_d and pieces[k][2] == off, (k, off)
            for kk in range(k, min(k + 1 + ahead, len(pieces))):
                if issued[kk] is None:
                    issue_piece(kk)
            return issued[k]

        def load_cols(w_d, c0, ahead=2):
            return next_piece(w_d, c0, ahead)

        def load_rows(w_d, r0):
            return next_piece(w_d, r0)

        bank_state = {"i": 0, "pool": list(range(8))}

        def next_bank():
            p = bank_state["pool"]
            b = p[bank_state["i"] % len(p)]
            bank_state["i"] += 1
            return b

        def set_banks(pool):
            bank_state["pool"] = list(pool)
            bank_state["i"] = 0

        evac_state = {"i": 0, "act_only": True}

        def evac_copy(out_ap, in_ap, reads, writes):
            evac_state["i"] += 1
            if evac_state["act_only"] or evac_state["i"] % 2:
                acopy(out_ap, in_ap, reads, writes)
            else:
                vcopy(out_ap, in_ap, reads, writes)

        def transpose_tile(src_bf, r_src, t):
            b = next_bank()
            pb = PS[b][:].bitcast(BF16)
            for c in range(KC):
                tr(pb[:, c * 128:(c + 1) * 128], src_bf[:, c * 128:(c + 1) * 128],
                   reads=(r_src if isinstance(r_src, list) else [r_src]) + [R_CB], writes=[RPS[b]])
            evac_copy(ACTT[:, :, t * 128:(t + 1) * 128], pb.rearrange("p (c t) -> p c t", c=KC),
                      reads=[], writes=[RPS[b], r_actt[t]])

        def gemm_feature_major(wv_, rw_, c, g, b):
            for kc in range(KC):
                mm(PS[b][:], wv_[:, kc, c * 128:(c + 1) * 128], ACTT[:, kc, g * 512:(g + 1) * 512],
                   kc == 0, kc == KC - 1, reads=[rw_] + r_actt[g * 4:(g + 1) * 4], writes=[RPS[b]])

        def gemm_token_major(wv_, rw_, t, b):
            for kc in range(KC):
                mm(PS[b][:], ACTT[:, kc, t * 128:(t + 1) * 128], wv_[:, kc, :],
                   kc == 0, kc == KC - 1, reads=[rw_, r_actt[t]], writes=[RPS[b]])

        m_mixer = A.mark()
        qkt_flat, r_qkt = A.bf16(16 * S_LEN, 16, "qkt")
        QKT = qkt_flat.rearrange("p (c t) -> p c t", c=16)
        v_flat, r_v = A.bf16(NT * 1536, NT, "v")
        V = v_flat.rearrange("p (t n) -> p t n", t=NT)
        xb = []
        for i in range(2):
            b_, rb_ = A.bf16(1024, 1, f"xb{i}")
            xb.append((b_, rb_[0]))
        m_proj = A.mark()
        if True:
            cos_t, r_cos = A.f32(S_LEN, 1, "cos")
            sin_t, r_sin = A.f32(S_LEN, 1, "sin")
            R_COS, R_SIN = r_cos[0], r_sin[0]
            scr = []
            for i in range(4):
                s_, rs_ = A.f32(512, 1, f"scr{i}")
                scr.append((s_, rs_[0]))
            q16_all, r_q16 = A.bf16(1024, 2, "q16")
            q16 = [(q16_all[:, 0:512], r_q16[0]), (q16_all[:, 512:1024], r_q16[1])]
            xb.append((scr[1][0].bitcast(BF16), [scr[1][1]]))
            xb.append((q16_all, list(r_q16)))
            posi, r_posi = A.i32(512, 1, "posi")
            vones = V[:, :, 512:1536].rearrange("p t (h e) -> p t h e", h=8)[:, :, :, 64:128]
            vmemset(vones, 1.0, r_v)
            two_pi = 2.0 * math.pi
            pf, rpf = scr[0]
            kf, rkf = scr[2]
            ki = scr[3][0].bitcast(I32)
            rki = scr[3][1]
            for g in range(NG):
                cs = slice(g * 512, (g + 1) * 512)
                dma("sp", posi, pos_d[:, cs], writes=[r_posi[0]])
                vcopy(pf, posi, [r_posi[0]], [rpf])
                ts(kf, pf, invf_lo, None, ALU.mult, None, [R_CF, rpf], [rkf])
                stt(pf, pf, invf, kf, ALU.mult, ALU.add, [R_CF, rkf], [rpf])
                for which, tab, rtab in ((0, sin_t, R_SIN), (1, cos_t, R_COS)):
                    shift = 0.0 if which == 0 else math.pi / 2
                    ang = tab[:, cs]
                    ts(ang, pf, shift, None, ALU.add, None, [rpf], [rtab])
                    ts(kf, ang, 1.0 / two_pi, None, ALU.mult, None, [rtab], [rkf])
                    vcopy(ki, kf, [rkf], [rki])
                    vcopy(kf, ki, [rki], [rkf])
                    stt(ang, kf, -CW1, ang, ALU.mult, ALU.add, [rkf], [rtab])
                    stt(ang, kf, -CW2, ang, ALU.mult, ALU.add, [rkf], [rtab])
                    ts(kf, ang, math.pi, -two_pi, ALU.is_gt, ALU.mult, [rtab], [rkf])
                    tt(ang, ang, kf, ALU.add, [rkf], [rtab])
                    ts(kf, ang, -math.pi, two_pi, ALU.is_lt, ALU.mult, [rtab], [rkf])
                    tt(ang, ang, kf, ALU.add, [rkf], [rtab])
            for t in range(NT):
                b_, rb_ = xb[t % 4]
                rb_ = rb_ if isinstance(rb_, list) else [rb_]
                dma("pool", b_, x_d[t * 128:(t + 1) * 128, :], writes=rb_)
                if t == 0:
                    issue_piece(0)
                    wva, rwva = issued[0]
                    piece_state["next"] = 1
                if t == 6:
                    issue_piece(1)
                if t == 10:
                    issue_piece(2)
                transpose_tile(b_, rb_, t)
                bq = next_bank()
                gemm_token_major(wva, rwva, t, bq)
                evac_copy(V[:, t, 0:512], PS[bq][:], [], [RPS[bq], r_v[t]])
            dump("dbg_xt", actt_flat, [128, KC * S_LEN], BF16, r_actt)

            rope_cnt = {"i": 0}

            def rope_finish(b, g, dst, rdst, qb_, rqb_):
                b2 = next_bank()
                mm(PS[b2][:], perm, qb_, True, True, reads=[rqb_, R_CB], writes=[RPS[b2]])
                k2 = 2 * (rope_cnt["i"] % 2)
                rope_cnt["i"] += 1
                t1, rt1 = scr[k2]
                t2, rt2 = scr[k2 + 1]
                cs = slice(g * 512, (g + 1) * 512)
                tt(t1, PS[b][:], cos_t[:, cs], ALU.mult, [R_COS], [RPS[b], rt1])
                tt(t2, PS[b2][:], sin_t[:, cs], ALU.mult, [R_SIN], [RPS[b2], rt2])
                S.op("pool", lambda e: e.tensor_tensor(dst, t1, t2, ALU.add), reads=[rt1, rt2], writes=[rdst])

            def proj_feature_major(w_d, c0, chunk0, rope):
                wv_, rw_ = load_cols(w_d, c0)
                pend = None
                n = 0
                for c in range(4):
                    for g in range(NG):
                        b = next_bank()
                        gemm_feature_major(wv_, rw_, c, g, b)
                        dst = QKT[:, chunk0 + c, g * 512:(g + 1) * 512]
                        rdst = r_qkt[chunk0 + c]
                        if not rope:
                            evac_copy(dst, PS[b][:], [], [RPS[b], rdst])
                            continue
                        if pend is not None:
                            rope_finish(*pend)
                        qb_, rqb_ = q16[n % 2]
                        n += 1
                        acopy(qb_, PS[b][:], [], [RPS[b], rqb_])
                        pend = (b, g, dst, rdst, qb_, rqb_)
                if pend is not None:
                    rope_finish(*pend)

            def proj_token_major_v(w_d, c0, ext):
                wv_, rw_ = load_cols(w_d, c0)
                for t in range(NT):
                    b = next_bank()
                    gemm_token_major(wv_, rw_, t, b)
                    if not ext:
                        evac_copy(V[:, t, 0:512], PS[b][:], [], [RPS[b], r_v[t]])
                    else:
                        dst = V[:, t, 512:1536].rearrange("p (h e) -> p h e", h=8)[:, :, 0:64]
                        src = PS[b][:].rearrange("p (h e) -> p h e", h=8)
                        evac_copy(dst, src, [], [RPS[b], r_v[t]])

            proj_token_major_v(w_in_d, 2560, True)
            proj_feature_major(w_in_d, 1536, 8, False)
            act(sin_t, sin_t, AF.Sin, [], [R_SIN])
            act(cos_t, cos_t, AF.Sin, [], [R_COS])
            ts(sin_t, sin_t, sgn, None, ALU.mult, None, [R_CF], [R_SIN])
            proj_feature_major(w_in_d, 2048, 12, False)
            evac_state["act_only"] = False
            proj_feature_major(w_in_d, 0, 0, True)
            proj_feature_major(w_in_d, 512, 4, True)
            dump("dbg_qkt", qkt_flat, [128, 16 * S_LEN], BF16, r_qkt)
            dump("dbg_v", v_flat, [128, NT * 1536], BF16, r_v)
        A.release(m_proj)

        if stop_after >= 2:
            fs = []
            for i in range(7):
                f_, rf_ = A.f32(512, 1, f"fs{i}")
                fs.append((f_, rf_[0]))
            bt_flat, r_bt = A.f32(8 * 256, 1, "bt")
            BT = bt_flat.rearrange("p (h j) -> p h j", h=8)
            R_BT = r_bt[0]
            pts = []
            ptp = []

            def alloc_ptp(i):
                p_, rp_ = A.bf16(1024, 1, f"ptp{i}")
                ptp.append((p_.rearrange("p (t n) -> p t n", t=2), rp_[0]))
                pts.append((p_[:, 0:512], rp_[0]))
                pts.append((p_[:, 512:1024], rp_[0]))
            alloc_ptp(0)
            alloc_ptp(1)
            btc_all, r_btc = A.f32(16, 1, "btc")
            btc = btc_all[:, 0:8]
            btc2 = btc_all[:, 8:16]

            def load_bias_tables():
                dma("sp", BT, bt_d.rearrange("h p j -> p h j"), writes=[R_BT])
                dma("sp", btc_all, btc_d, writes=[R_BT])
                for h_ in range(8):
                    ts(BT[:, h_, :], BT[:, h_, :], btc[:, h_:h_ + 1], None, ALU.subtract, None, [], [R_BT])
                ts(bt_flat, bt_flat, 8.0, None, ALU.mult, None, [], [R_BT])
            lamt, r_lam = A.f32(256, 1, "lam")
            dma("sp", lamt, lam_d, writes=[r_lam[0]])
            gcol, r_g = A.f32(1, 1, "gcol")
            R_G = r_g[0]
            dma("sp", gcol, subg_d, writes=[R_G])
            lp, r_lp = A.f32(128, 1, "lp")
            neglam = smallf[:, 0:1]
            e1 = smallf[:, 1:2]
            e2 = smallf[:, 2:3]
            lp2 = lp.rearrange("p (a d) -> p a d", a=2)
            l4 = lamt.rearrange("p (a d) -> p a d", a=4)
            tt(lp2[:, 0, :], l4[:, 0, :], l4[:, 1, :], ALU.mult, [r_lam[0]], [r_lp[0]])
            tt(lp2[:, 1, :], l4[:, 2, :], l4[:, 3, :], ALU.mult, [r_lam[0]], [r_lp[0]])
            S.op("dve", lambda e: e.reduce_sum(e1, lp2[:, 0, :], mybir.AxisListType.X),
                 reads=[r_lp[0]], writes=[R_SM])
            S.op("dve", lambda e: e.reduce_sum(e2, lp2[:, 1, :], mybir.AxisListType.X),
                 reads=[r_lp[0]], writes=[R_SM])
            act(e1, e1, AF.Exp, [], [R_SM])
            act(e2, e2, AF.Exp, [], [R_SM])
            stt(neglam, e2, -LAMBDA_INIT, e1, ALU.add, ALU.subtract, [], [R_SM])
            ts(gcol, gcol, 1.0 - LAMBDA_INIT, None, ALU.mult, None, [], [R_G])
            alloc_ptp(2)

            def diff_qk_burst(h, G, kt):
                q0 = G * 512
                r = kt - 4 * G
                a = 0 if r < 0 else 128 * r
                bs = (4 + 2 * (kt % 2), 5 + 2 * (kt % 2))
                for t in range(2):
                    rows = slice(t * 64, (t + 1) * 64)
                    mm(PS[bs[t]][:, a:512], QKT[rows, 4 + h, kt * 128:(kt + 1) * 128],
                       QKT[rows, h, q0 + a:q0 + 512], True, True, reads=[r_qkt[4 + h], r_qkt[h]],
                       writes=[RPS[bs[0]], RPS[bs[1]]] if t == 0 else [RPS[bs[1]]])

            def diff_exps(h, G, kt):
                r = kt - 4 * G
                sl = kt % 2
                pp = pspair(4 + 2 * sl)
                pt2, rpt = ptp[sl]
                wr = [RPS[4 + 2 * sl], RPS[5 + 2 * sl], rpt]
                if r < 0:
                    act(pt2, pp, AF.Exp, [], wr, scale=0.125)
                    return
                a = 128 * r
                act(pt2[:, :, a:a + 64], pp[:, :, a:a + 64], AF.Exp, [R_CF], wr, scale=0.125, bias=maskb)
                if a + 64 < 512:
                    act(pt2[:, :, a + 64:512], pp[:, :, a + 64:512], AF.Exp, [], wr, scale=0.125)

            def diff_av_burst(h, G, kt, first):
                r = kt - 4 * G
                a = 0 if r < 0 else 128 * r
                rp = [pts[2 * (kt % 2)][1], pts[2 * (kt % 2) + 1][1]]
                n = 0
                for t in range(2):
                    pt, rpt = pts[2 * (kt % 2) + t]
                    for (bank, lhs) in ((t, V[:, kt, h * 128:(h + 1) * 128]), (2 + t, ones_bf)):
                        st_ = first[bank]
                        first[bank] = False
                        mm(PS[bank][:, a:512], lhs, pt[:, a:512], st_, False,
                           reads=(rp if n == 0 else [rpt]) + [r_v[kt], R_OB],
                           writes=[RPS[0], RPS[1], RPS[2], RPS[3]] if (n == 0 and not st_) else [RPS[bank]],
                           skip=True)
                        n += 1

            def diff_epilogue1(h, G, par):
                lz0, rlz0 = fs[0]
                lz1, rlz1 = fs[1]
                o0, ro0 = fs[2 + 4 * par]
                o1, ro1 = fs[3]
                act(lz0, PS[2][:], AF.Ln, [], [RPS[2], rlz0])
                act(lz1, PS[3][:], AF.Ln, [], [RPS[3], rlz1])
                vcopy(o0, PS[0][:], [], [RPS[0], ro0])
                vcopy(o1, PS[1][:], [], [RPS[1], ro1])

            def diff_epilogue2a(h, G, par):
                lz0, rlz0 = fs[0]
                lz1, rlz1 = fs[1]
                o0, ro0 = fs[2 + 4 * par]
                o1, ro1 = fs[3]
                osq, rosq = fs[4]
                osq_b = osq.bitcast(BF16)[:, 0:512]
                act(lz0, lz0, AF.Exp, [], [rlz0], scale=-1.0)
                act(lz1, lz1, AF.Exp, [], [rlz1], scale=-1.0)
                tt(o0, o0, lz0, ALU.mult, [rlz0], [ro0])
                tt(o1, o1, lz1, ALU.mult, [rlz1], [ro1])
                stt(o0, o1, neglam, o0, ALU.mult, ALU.add, [ro1, R_SM], [ro0])
                tt(osq_b, o0, o0, ALU.mult, [ro0], [rosq])

            def diff_epilogue2b(h, G, par, bk):
                q0 = G * 512
                rs_, rrs_ = fs[5]
                o0, ro0 = fs[2 + 4 * par]
                osq, rosq = fs[4]
                osq_b = osq.bitcast(BF16)[:, 0:512]
                mm(PS[bk][:], ones_bf, osq_b, True, True, reads=[rosq, R_OB], writes=[RPS[bk]])
                act(rs_, PS[bk][:], AF.Ln, [R_CF], [RPS[bk], rrs_], scale=1.0 / 128.0, bias=epsc)
                act(rs_, rs_, AF.Exp, [], [rrs_], scale=-0.5)
                stt(ACTT[:, h, q0:q0 + 512], o0, gcol, rs_, ALU.mult, ALU.mult, [ro0, rrs_, R_G],
                    r_actt[G * 4:(G + 1) * 4])

            deferred = None
            dcount = 0
            for h in range(4):
                for G in range(NG):
                    first = {0: True, 1: True, 2: True, 3: True}
                    nkt = 4 * G + 4
                    diff_qk_burst(h, G, 0)
                    for kt in range(nkt):
                        if kt + 1 < nkt:
                            diff_qk_burst(h, G, kt + 1)
                        diff_exps(h, G, kt)
                        if deferred is not None:
                            if kt == 1:
                                diff_epilogue2a(*deferred)
                            if kt == min(5, nkt - 1):
                                diff_epilogue2b(*deferred, 4 + 2 * (kt % 2))
                                deferred = None
                        diff_av_burst(h, G, kt, first)
                    diff_epilogue1(h, G, dcount % 2)
                    deferred = (h, G, dcount % 2)
                    dcount += 1
                    if dcount == 2:
                        load_bias_tables()
            diff_epilogue2a(*deferred)
            diff_epilogue2b(*deferred, 6)

            for t_ in range(8):
                dma("sp", qkt_flat[:, t_ * 2048:(t_ + 1) * 2048].bitcast(F32), x_d[t_ * 128:(t_ + 1) * 128, :],
                    writes=[r_qkt[t_]])

            def chunk_geom(G, kt):
                q0 = G * 512
                return max(0, q0 - 128 * kt), min(640, q0 + 512 - 128 * kt)

            def chunk_qk_pair(hp, G, kt, slot):
                j0, j1 = chunk_geom(G, kt)
                bs = (2 + 2 * slot, 3 + 2 * slot)
                if WARM_FILL:
                    mm(PS[6][:], ones_bf, V[:, kt, 0:512], True, True, reads=[R_OB, r_v[kt]], writes=[RPS[6]])
                for i in range(2):
                    rows = slice(i * 64, i * 64 + 64)
                    mm(PS[bs[i]][:, 0:j1 - j0], QKT[rows, 12 + hp, kt * 128:(kt + 1) * 128],
                       QKT[rows, 8 + hp, 128 * kt + j0:128 * kt + j1], True, True,
                       reads=[r_qkt[12 + hp], r_qkt[8 + hp]],
                       writes=[RPS[bs[0]], RPS[bs[1]]] if i == 0 else [RPS[bs[1]]])

            def chunk_exp_pair(hp, G, kt, slot):
                j0, j1 = chunk_geom(G, kt)
                w = j1 - j0
                wa = max(0, min(j1, 256) - j0)
                wm = min(j1, 576) - j0
                pp = pspair(2 + 2 * slot)
                pt2, rpt = ptp[slot]
                wr = [RPS[2 + 2 * slot], RPS[3 + 2 * slot]]
                if wa > 0:
                    tt(pp[:, :, 0:wa], pp[:, :, 0:wa], BT[:, 2 * hp:2 * hp + 2, j0:j0 + wa], ALU.add, [R_BT], wr)
                act(pt2[:, :, 0:wm], pp[:, :, 0:wm], AF.Exp, [], wr + [rpt], scale=0.125)
                if wm < w:
                    act(pt2[:, :, wm:w], pp[:, :, wm:w], AF.Exp, [R_CF], wr + [rpt], scale=0.125, bias=maskb2)

            def chunk_av_pair(hp, G, kt, slot, first):
                q0 = G * 512
                j0, j1 = chunk_geom(G, kt)
                c0 = 128 * kt + j0 - q0
                pt2, rpt = ptp[slot]
                for i in range(2):
                    h = 2 * hp + i
                    st_ = first[i]
                    first[i] = False
                    mm(PS[i][:, c0:c0 + j1 - j0], V[:, kt, 512 + h * 128:512 + (h + 1) * 128],
                       pt2[:, i, 0:j1 - j0], st_, False, reads=[rpt, r_v[kt]],
                       writes=[RPS[0], RPS[1]] if i == 0 else [RPS[i]], skip=True)

            def chunk_epilogue_pair(hp, G, par):
                q0 = G * 512
                for i in range(2):
                    rows = slice(i * 64, i * 64 + 64)
                    rz, rrz = fs[2 * par + i]
                    act(rz[0:64, :], PS[i][64:128, :], AF.Ln, [], [RPS[i], rrz])
                    act(rz[0:64, :], rz[0:64, :], AF.Exp, [], [rrz], scale=-1.0)
                    tt(ACTT[rows, 4 + hp, q0:q0 + 512], PS[i][0:64, :], rz[0:64, :], ALU.mult,
                       [rrz], [RPS[i]] + r_actt[G * 4:(G + 1) * 4])

            WARM_FILL = False
            NSL = 2 if WARM_FILL else 3
            gcount = 0
            bcount = 0
            for hp in range(4):
                for G in range(NG):
                    kts = list(range(max(0, 4 * G - 4), 4 * G + 4))
                    first = [True, True]
                    n = len(kts)
                    slots = [(bcount + i) % NSL for i in range(n)]
                    bcount += n
                    LA = NSL - 1
                    for i in range(min(LA, n)):
                        chunk_qk_pair(hp, G, kts[i], slots[i])
                    for i, kt in enumerate(kts):
                        if i + LA < n:
                            chunk_qk_pair(hp, G, kts[i + LA], slots[i + LA])
                        chunk_exp_pair(hp, G, kt, slots[i])
                        chunk_av_pair(hp, G, kt, slots[i], first)
                    chunk_epilogue_pair(hp, G, gcount % 2)
                    gcount += 1
            set_banks(range(8))
            dump("dbg_yt", actt_flat, [128, KC * S_LEN], BF16, r_actt)
        A.release(m_mixer)

        if stop_after >= 3:
            x_flat, r_x = A.f32(NT * D, NT, "X")
            X = x_flat.rearrange("p (t n) -> p t n", t=NT)
            gb_flat, r_gb = A.f32(2 * D, 1, "gb")
            GB = gb_flat.rearrange("p (a n) -> p a n", a=2)
            R_GB = r_gb[0]
            xs16 = []
            for i in range(2):
                b_, rb_ = A.bf16(D, 1, f"xs16{i}")
                xs16.append((b_, rb_[0]))
            stats_t, r_stats = A.f32(64, 3, "stats")
            lnc = {"i": 0}

            NSLOT = 3
            SW = 20

            def ln_s1(t):
                i = lnc["i"] % NSLOT
                lnc["i"] += 1
                rs = r_stats[i]
                st6 = stats_t[:, i * SW:i * SW + 12].rearrange("p (a b) -> p a b", a=2)
                mv = stats_t[:, i * SW + 12:i * SW + 14]
                negmean = stats_t[:, i * SW + 15:i * SW + 16]
                xt = X[:, t, :]
                S.op("dve", lambda e: e.bn_stats(st6[:, 0, :], xt[:, 0:512]), reads=[r_x[t]], writes=[rs])
                S.op("dve", lambda e: e.bn_stats(st6[:, 1, :], xt[:, 512:1024]), reads=[r_x[t]], writes=[rs])
                S.op("dve", lambda e: e.bn_aggr(mv, st6), reads=[], writes=[rs])
                ts(negmean, mv[:, 0:1], -1.0, None, ALU.mult, None, [], [rs])
                return i

            def ln_s2(t, i):
                rs = r_stats[i]
                mv = stats_t[:, i * SW + 12:i * SW + 14]
                rstd = stats_t[:, i * SW + 14:i * SW + 15]
                negmean = stats_t[:, i * SW + 15:i * SW + 16]
                nmr = stats_t[:, i * SW + 16:i * SW + 17]
                xt = X[:, t, :]
                act(rstd, mv[:, 1:2], AF.Ln, [R_CF], [rs], bias=epsc)
                act(rstd, rstd, AF.Exp, [], [rs], scale=-0.5)
                act(nmr, negmean, AF.Identity, [], [rs], scale=rstd)
                act(xt, xt, AF.Identity, [rs], [r_x[t]], scale=rstd, bias=nmr)

            def ln_gb(t, out_dst=None):
                xt = X[:, t, :]
                if out_dst is not None and t % 2 == 1:
                    S.op("pool", lambda e: e.tensor_tensor(xt, xt, GB[:, 0, :], ALU.mult), reads=[R_GB], writes=[r_x[t]])
                else:
                    tt(xt, xt, GB[:, 0, :], ALU.mult, [R_GB], [r_x[t]])
                S.op("pool", lambda e: e.tensor_tensor(xt, xt, GB[:, 1, :], ALU.add), reads=[R_GB], writes=[r_x[t]])
                if out_dst is not None:
                    out_dmas.append(dma("sp", out_dst, xt, reads=[r_x[t]]))

            def ln_cast(t):
                xb_, rxb_ = xs16[t % 2]
                acopy(xb_, X[:, t, :], [r_x[t]], [rxb_])

            def ln_tr(t):
                xb_, rxb_ = xs16[t % 2]
                prev = evac_state["act_only"]
                evac_state["act_only"] = True
                transpose_tile(xb_, rxb_, t)
                evac_state["act_only"] = prev

            class LNPipe:
                def __init__(self, to_actt, dst_fn=None):
                    self.to_actt = to_actt
                    self.dst_fn = dst_fn
                    self.q = []

                def _advance(self):
                    keep = []
                    for ent in self.q:
                        t, i, stg = ent
                        if stg == 0:
                            ln_s2(t, i)
                        elif stg == 1:
                            ln_gb(t, None if self.dst_fn is None else self.dst_fn(t))
                        elif stg == 2:
                            ln_cast(t)
                        elif stg == 3:
                            ln_tr(t)
                        ent[2] += 1
                        if ent[2] < (4 if self.to_actt else 2):
                            keep.append(ent)
                    self.q = keep

                def push(self, t):
                    i = ln_s1(t)
                    self._advance()
                    self.q.append([t, i, 0])

                def flush(self, fillers=()):
                    fillers = list(fillers)
                    while self.q:
                        self._advance()
                        if fillers:
                            fillers.pop(0)()
                    for f_ in fillers:
                        f_()

            def load_ln(idx):
                dma("sp", GB, lnp_d[idx].rearrange("a p n -> p a n"), writes=[R_GB])

            def out_proj(w_d, ln_idx, first_from_hbm, after_gemms=None, at_tile=None, drain=()):
                load_ln(ln_idx)
                xin = []
                if first_from_hbm:
                    for i in range(3):
                        b_, rb_ = A.f32(512, 1, f"xin{i}")
                        xin.append((b_, rb_[0]))
                cnt = 0
                pipe = LNPipe(True)
                ws = [load_cols(w_d, 0), load_cols(w_d, 512, ahead=1)]
                for t in range(NT):
                    for j in range(2):
                        wv_, rw_ = ws[j]
                        b = next_bank()
                        gemm_token_major(wv_, rw_, t, b)
                        dst = X[:, t, j * 512:(j + 1) * 512]
                        if first_from_hbm and t >= 8:
                            xi, rxi = xin[cnt % 3]
                            cnt += 1
                            dma("sp", xi, x_d[t * 128:(t + 1) * 128, j * 512:(j + 1) * 512], writes=[rxi])
                            stt(dst, xi, ALPHA, PS[b][:], ALU.mult, ALU.add, [rxi], [RPS[b], r_x[t]])
                        else:
                            stt(dst, dst, ALPHA, PS[b][:], ALU.mult, ALU.add, [], [RPS[b], r_x[t]])
                    pipe.push(t)
                    if at_tile is not None and t in at_tile:
                        at_tile[t]()
                if after_gemms is not None:
                    after_gemms()
                pipe.flush(drain)

            memt_flat, r_memt = A.bf16(KC * 256, 1, "memt")
            MEMT = memt_flat.rearrange("p (c t) -> p c t", c=KC)
            mb_flat, r_mb = A.bf16(2 * D, 1, "memb")
            MB = mb_flat.rearrange("p (t n) -> p t n", t=2)
            R_MEMT, R_MB = r_memt[0], r_mb[0]
            dma("pool", MB, mem_d.rearrange("(t p) n -> p t n", p=128), writes=[R_MB])

            def mem_transposes():
                for mt in range(2):
                    b = next_bank()
                    pb = PS[b][:].bitcast(BF16)
                    for c in range(KC):
                        tr(pb[:, c * 128:(c + 1) * 128], MB[:, mt, c * 128:(c + 1) * 128],
                           reads=[R_MB, R_CB], writes=[RPS[b]])
                    evac_copy(MEMT[:, :, mt * 128:(mt + 1) * 128], pb.rearrange("p (c t) -> p c t", c=KC),
                              [], [RPS[b], R_MEMT])
            m4 = A.mark()
            qmt_flat, r_qmt = A.bf16(KC * S_LEN, KC, "qmt")
            QMT = qmt_flat.rearrange("p (c t) -> p c t", c=KC)
            kmt_flat, r_kmt = A.bf16(KC * 256, 1, "kmt")
            KMT = kmt_flat.rearrange("p (c t) -> p c t", c=KC)
            vm_flat, r_vm = A.bf16(2 * D, 1, "vm")
            VM = vm_flat.rearrange("p (t n) -> p t n", t=2)
            R_KMT, R_VM = r_kmt[0], r_vm[0]

            qstate = {}

            def q_units(j, g):
                wv_, rw_ = qstate[j]
                for c in range(4):
                    b = next_bank()
                    gemm_feature_major(wv_, rw_, c, g, b)
                    evac_copy(QMT[:, j * 4 + c, g * 512:(g + 1) * 512], PS[b][:], [],
                              [RPS[b], r_qmt[j * 4 + c]])

            def q_fill0():
                qstate[0] = next_piece(w_mq_d, 0, ahead=0)
                q_units(0, 0)

            def k_proj(j):
                wv_, rw_ = next_piece(w_mk_d, j * 512, ahead=1 - j)
                for c in range(4):
                    b = next_bank()
                    for kc in range(KC):
                        mm(PS[b][:, 0:256], wv_[:, kc, c * 128:(c + 1) * 128], MEMT[:, kc, :],
                           kc == 0, kc == KC - 1, reads=[rw_, R_MEMT], writes=[RPS[b]])
                    evac_copy(KMT[:, j * 4 + c, :], PS[b][:, 0:256], [], [RPS[b], R_KMT])

            m3 = A.mark()
            out_proj(w_o_d, 0, True, after_gemms=lambda: (issue_piece(9), issue_piece(10)),
                     at_tile={6: mem_transposes, 8: q_fill0, 12: lambda: q_units(0, 1)},
                     drain=[lambda: q_units(0, 2), lambda: k_proj(0), lambda: k_proj(1)])
            A.release(m3)
            dump("dbg_x1", x_flat, [128, NT * D], F32, r_x)

        if stop_after >= 4:
            q_units(0, 3)
            for j in range(2):
                wv_, rw_ = load_cols(w_mv_d, j * 512)
                for mt in range(2):
                    b = next_bank()
                    for kc in range(KC):
                        mm(PS[b][:], MEMT[:, kc, mt * 128:(mt + 1) * 128], wv_[:, kc, :],
                           kc == 0, kc == KC - 1, reads=[rw_, R_MEMT], writes=[RPS[b]])
                    evac_copy(VM[:, mt, j * 512:(j + 1) * 512], PS[b][:], [], [RPS[b], R_VM])
            qstate[1] = load_cols(w_mq_d, 512)
            for g in range(NG):
                q_units(1, g)
            ptm = []
            for i in range(4):
                p_, rp_ = A.bf16(512, 1, f"ptm{i}")
                ptm.append((p_, rp_[0]))
            rzs = []
            for i in range(2):
                f_, rf_ = A.f32(512, 1, f"rzm{i}")
                rzs.append((f_, rf_[0]))
            def cross_scores(h, G, idx):
                cs = slice(G * 512, (G + 1) * 512)
                pp = []
                for mt in range(2):
                    b = next_bank()
                    for dc in range(2):
                        mm(PS[b][:], KMT[:, 2 * h + dc, mt * 128:(mt + 1) * 128], QMT[:, 2 * h + dc, cs],
                           dc == 0, dc == 1, reads=[R_KMT, r_qmt[2 * h + dc]], writes=[RPS[b]])
                    pt, rpt = ptm[(2 * idx + mt) % 4]
                    act(pt, PS[b][:], AF.Exp, [], [RPS[b], rpt], scale=1.0 / 16.0)
                    pp.append((pt, rpt))
                return pp

            def cross_values(h, G, idx, pp):
                cs = slice(G * 512, (G + 1) * 512)
                bz = next_bank()
                for mt in range(2):
                    mm(PS[bz][:], ones_bf, pp[mt][0], mt == 0, mt == 1,
                       reads=[pp[mt][1], R_OB], writes=[RPS[bz]])
                rz, rrz = rzs[idx % 2]
                act(rz, PS[bz][:], AF.Ln, [], [RPS[bz], rrz])
                act(rz, rz, AF.Exp, [], [rrz], scale=-1.0)
                for ec in range(2):
                    bo = next_bank()
                    for mt in range(2):
                        mm(PS[bo][:], VM[:, mt, h * 256 + ec * 128:h * 256 + (ec + 1) * 128], pp[mt][0],
                           mt == 0, mt == 1, reads=[pp[mt][1], R_VM], writes=[RPS[bo]])
                    tt(ACTT[:, 2 * h + ec, cs], PS[bo][:], rz, ALU.mult, [rrz],
                       [RPS[bo]] + r_actt[G * 4:(G + 1) * 4])

            groups = [(h, G) for h in range(4) for G in range(NG)]
            pend = cross_scores(groups[0][0], groups[0][1], 0)
            for i, (h, G) in enumerate(groups):
                nxt = None
                if i + 1 < len(groups):
                    nxt = cross_scores(groups[i + 1][0], groups[i + 1][1], i + 1)
                cross_values(h, G, i, pend)
                pend = nxt
            dump("dbg_ot", actt_flat, [128, KC * S_LEN], BF16, r_actt)
            A.release(m4)
            hts = []
            for i in range(2):
                h_, rh_ = A.bf16(4 * S_LEN, NT, f"ht{i}", fine=False)
                hts.append((h_.rearrange("p (c t) -> p c t", c=4), rh_))
            rl = []
            for i in range(3):
                f_, rf_ = A.f32(512, 1, f"relu{i}")
                rl.append((f_, rf_[0]))
            rcs = {"i": 0}
            up0 = {}

            def mlp_up(fg, wv_, rw_, HT, r_ht, g):
                for fc in range(4):
                    b = next_bank()
                    gemm_feature_major(wv_, rw_, fc, g, b)
                    r_, rr_ = rl[rcs["i"] % 3]
                    rcs["i"] += 1
                    act(r_, PS[b][:], AF.Relu, [], [RPS[b], rr_])
                    tt(HT[:, fc, g * 512:(g + 1) * 512], r_, r_, ALU.mult, [rr_], r_ht[g * 4:(g + 1) * 4])

            def mlp_down(fg, wd, rwd, HT, r_ht, t):
                for half in range(2):
                    b = next_bank()
                    for fc in range(4):
                        mm(PS[b][:], HT[:, fc, t * 128:(t + 1) * 128], wd[:, fc, half * 512:(half + 1) * 512],
                           fc == 0, fc == 3, reads=[rwd, r_ht[t]], writes=[RPS[b]])
                    dst = X[:, t, half * 512:(half + 1) * 512]
                    if fg == 0:
                        stt(dst, dst, ALPHA, PS[b][:], ALU.mult, ALU.add, [], [RPS[b], r_x[t]])
                    else:
                        tt(dst, dst, PS[b][:], ALU.add, [], [RPS[b], r_x[t]])
                if fg == 7:
                    pipe3.push(t)

            def up_fill(g):
                if "w" not in up0:
                    up0["w"] = next_piece(w_up_d, 0, ahead=0)
                mlp_up(0, up0["w"][0], up0["w"][1], hts[0][0], hts[0][1], g)

            def down_fill(t0, t1):
                if "d" not in up0:
                    up0["d"] = next_piece(w_dn_d, 0, ahead=1)
                for t in range(t0, t1):
                    mlp_down(0, up0["d"][0], up0["d"][1], hts[0][0], hts[0][1], t)

            out_proj(w_mo_d, 1, False, after_gemms=lambda: (issue_piece(17), issue_piece(18)),
                     at_tile={8: lambda: up_fill(0), 12: lambda: up_fill(1)},
                     drain=[lambda: up_fill(2), lambda: down_fill(0, 4), lambda: down_fill(4, 8)])
            dump("dbg_x2", x_flat, [128, NT * D], F32, r_x)

        if stop_after >= 6:
            load_ln(2)
            pipe3 = LNPipe(False, lambda t_: out_d[t_ * 128:(t_ + 1) * 128, :])
            x3_, rx3_ = A.bf16(4096, 1, "ring3")
            ring.append((x3_, rx3_[0]))

            def mlp_down2(wds, HTs, t):
                for half in range(2):
                    b = next_bank()
                    n = 0
                    for (wd, rwd), (HT, r_ht) in zip(wds, HTs):
                        for fc in range(4):
                            mm(PS[b][:], HT[:, fc, t * 128:(t + 1) * 128], wd[:, fc, half * 512:(half + 1) * 512],
                               n == 0, n == 7, reads=[rwd, r_ht[t]], writes=[RPS[b]])
                            n += 1
                    dst = X[:, t, half * 512:(half + 1) * 512]
                    tt(dst, dst, PS[b][:], ALU.add, [], [RPS[b], r_x[t]])
                pipe3.push(t)

            up_fill(3)
            down_fill(8, NT)
            for fg in range(1, 6):
                HT, r_ht = hts[fg % 2]
                wv_, rw_ = load_cols(w_up_d, fg * 512)
                for g in range(NG):
                    mlp_up(fg, wv_, rw_, HT, r_ht, g)
                wd, rwd = load_rows(w_dn_d, fg * 512)
                for t in range(NT):
                    mlp_down(fg, wd, rwd, HT, r_ht, t)
            wu6 = load_cols(w_up_d, 6 * 512)
            wd6 = load_rows(w_dn_d, 6 * 512)
            wu7 = load_cols(w_up_d, 7 * 512)
            wd7 = load_rows(w_dn_d, 7 * 512)
            mlp_up(6, wu6[0], wu6[1], hts[0][0], hts[0][1], 0)
            mlp_up(7, wu7[0], wu7[1], hts[1][0], hts[1][1], 0)
            for g in range(NG):
                if g + 1 < NG:
                    mlp_up(6, wu6[0], wu6[1], hts[0][0], hts[0][1], g + 1)
                    mlp_up(7, wu7[0], wu7[1], hts[1][0], hts[1][1], g + 1)
                for t in range(4 * g, 4 * g + 4):
                    mlp_down2((wd6, wd7), (hts[0], hts[1]), t)
            pipe3.flush()

        if stop_after < 6:
            z_, rz_ = A.f32(D, 1, "zero")
            vmemset(z_, 0.0, [rz_[0]])
            for t in range(NT):
                out_dmas.append(dma("sp", out_d[t * 128:(t + 1) * 128, :], z_, reads=[rz_[0]]))
        S.fence("sp", out_dmas)
        S.emit(st)
    return nc


def _host_constants():
    p = np.arange(128)
    j = (p % 64) % 32
    invf64 = 1.0 / (10000.0 ** (np.arange(0, 64, 2, dtype=np.float64) / 64.0))
    invf = invf64.astype(np.float32)
    invf_lo = (invf64 - invf.astype(np.float64)).astype(np.float32)
    cf = np.zeros((128, 8), np.float32)
    cf[:, 0] = invf[j]
    cf[:, 5] = invf_lo[j]
    cf[:, 1] = np.where((p % 64) < 32, -1.0, 1.0)
    cf[:, 2] = LN_EPS
    cf[:, 3] = np.where(p < 64, 0.0, MASK_NEG)
    cf[:, 4] = np.where(p < 64, MASK_NEG, 0.0)
    ident = np.eye(128, dtype=np.float32)
    partner = np.where((p % 64) < 32, p + 32, p - 32)
    perm = np.zeros((128, 128), np.float32)
    perm[partner, p] = 1.0
    cb = np.concatenate([ident, perm], axis=1).astype(ml_dtypes.bfloat16)
    i = np.arange(128)[:, None]
    jj = np.arange(256)[None, :]
    rel_idx = np.clip(jj - i, -128, 128) + 128
    return cf, cb, rel_idx


def _prep(x, mem, positions, w_in, diff_lambda, subln_g, rel_bias, w_o, ln1_g, ln1_b,
          w_mq, w_mk, w_mv, w_mo, ln2_g, ln2_b, w_up, w_down, ln3_g, ln3_b):
    n = 8
    f = lambda a: np.ascontiguousarray(np.asarray(a), dtype=np.float32)
    x = f(x); mem = f(mem)
    positions = np.ascontiguousarray(np.asarray(positions), dtype=np.int32)
    cf, cb, rel_idx = _host_constants()
    rb = f(rel_bias)[0]
    ii = np.arange(128)[:, None]
    jj = np.arange(256)[None, :]
    corner = np.broadcast_to((ii >= 64) & (jj < 64), (8, 128, 256))
    bt = np.ascontiguousarray(np.where(corner, np.float32(MASK_NEG), rb[:, rel_idx]), dtype=np.float32)
    c_h = np.broadcast_to(rb[:, 256][None, :], (128, 8))
    c_h2 = np.where(np.arange(128)[:, None] < 64, np.float32(MASK_NEG), c_h)
    btc = np.ascontiguousarray(np.concatenate([c_h, c_h2], axis=1), dtype=np.float32)
    lamb = np.ascontiguousarray(np.broadcast_to(f(diff_lambda)[0].reshape(1, 256), (128, 256)))
    subg = np.ascontiguousarray(f(subln_g)[0].reshape(128, 1))
    lnp = np.stack([np.stack([np.broadcast_to(f(g)[0][None, :], (128, D)),
                              np.broadcast_to(f(b)[0][None, :], (128, D))])
                    for g, b in ((ln1_g, ln1_b), (ln2_g, ln2_b), (ln3_g, ln3_b))])
    lnp = np.ascontiguousarray(lnp, dtype=np.float32)
    shared = {
        "w_in": f(w_in)[0], "w_o": f(w_o)[0], "w_mq": f(w_mq)[0], "w_mk": f(w_mk)[0],
        "w_mv": f(w_mv)[0], "w_mo": f(w_mo)[0], "w_up": f(w_up)[0], "w_down": f(w_down)[0],
        "lamb": lamb, "subg": subg, "bt": bt, "btc": btc, "lnp": lnp, "cf32": cf, "cbf": cb,
    }
    in_maps = []
    for b in range(n):
        m = dict(shared)
        m["x"] = x[b]
        m["mem"] = mem[b]
        m["posb"] = np.ascontiguousarray(np.broadcast_to(positions[b][None, :], (128, S_LEN)))
        in_maps.append(m)
    return in_maps


_CACHE = {}


def kernel(**inputs):
    in_maps = _prep(**inputs)
    if "nc" not in _CACHE:
        _CACHE["nc"] = build_program()
    res = run_bass_kernel_spmd(_CACHE["nc"], in_maps, core_ids=list(range(8)))
    return np.stack([r["out"] for r in res.results], axis=0).astype(np.float32)
```

```python
import math
from contextlib import ExitStack

import numpy as np
import ml_dtypes

import concourse.bass as bass
import concourse.mybir as mybir
from concourse.bass_utils import run_bass_kernel_spmd

F32 = mybir.dt.float32
BF16 = mybir.dt.bfloat16
I32 = mybir.dt.int32
AF = mybir.ActivationFunctionType
ALU = mybir.AluOpType

S_LEN = 2048
D = 1024
NT = 16
NG = 4
KC = 8
ALPHA = 2.0 ** 0.25
LAMBDA_INIT = 0.8 - 0.6 * math.exp(0.0)
LN_EPS = 1e-5
MASK_NEG = -1.0e4
CW1 = 6.28125
CW2 = 2.0 * math.pi - 6.28125
ENGS = ("pe", "act", "dve", "pool", "sp")


class Res:
    __slots__ = ("name", "w", "r")

    def __init__(self, name=""):
        self.name = name
        self.w = None
        self.r = {}


class Op:
    __slots__ = ("eng", "fn", "deps", "dma", "sig", "sem", "val", "slot", "gidx")

    def __init__(self, eng, fn, dma):
        self.eng = eng
        self.fn = fn
        self.dma = dma
        self.deps = ()
        self.sig = False
        self.sem = None
        self.val = 0
        self.slot = -1


class Sched:
    def __init__(self, nc, n_dma_sems=None):
        self.nc = nc
        self.ops = []
        self.eng_ops = {e: [] for e in ENGS}
        self.n_dma_sems = n_dma_sems or {"sp": 32, "pool": 12, "act": 2}
        self.dma_base = {}
        base = 0
        for q_, n_ in self.n_dma_sems.items():
            self.dma_base[q_] = base
            base += n_
        self.n_dma_total = base
        self.dma_count = {q_: 0 for q_ in self.n_dma_sems}
        self.dma_hist = {}

    def _add(self, eng, fn, reads, writes, dma):
        o = Op(eng, fn, dma)
        deps = set()
        for r in reads:
            if r.w is not None:
                deps.add(r.w)
        for w in writes:
            if w.w is not None:
                deps.add(w.w)
            deps.update(w.r.values())
        if eng == "pe" and not dma:
            deps = {d for d in deps if d.dma or d.eng != "pe"}
        if dma:
            slot = self.dma_base[eng] + self.dma_count[eng] % self.n_dma_sems[eng]
            self.dma_count[eng] += 1
            o.slot = slot
            prev = self.dma_hist.get(slot)
            if prev is not None:
                deps.add(prev)
                o.val = prev.val + 16
            else:
                o.val = 16
            self.dma_hist[slot] = o
            o.sig = True
        o.deps = deps
        for d in deps:
            d.sig = True
        key = o if dma else eng
        for r in reads:
            r.r[key] = o
        for w in writes:
            w.w = o
            w.r = {}
        o.gidx = len(self.ops)
        self.ops.append(o)
        self.eng_ops[eng].append(o)
        return o

    def op(self, eng, fn, reads=(), writes=()):
        return self._add(eng, fn, reads, writes, False)

    def dma(self, queue, fn, reads=(), writes=()):
        return self._add(queue, fn, reads, writes, True)

    def fence(self, eng, after):
        o = self._add(eng, lambda e: e.nop(), (), (), False)
        for d in after:
            d.sig = True
        o.deps = set(o.deps) | set(after)
        return o

    def emit(self, stack):
        nc = self.nc
        esem = {e: stack.enter_context(nc.semaphore(f"s_{e}")) for e in ENGS}
        dsem = [stack.enter_context(nc.semaphore(f"s_dma{i}")) for i in range(self.n_dma_total)]
        for e in ENGS:
            c = 0
            for o in self.eng_ops[e]:
                if o.dma:
                    o.sem = dsem[o.slot]
                elif o.sig:
                    c += 1
                    o.sem = esem[e]
                    o.val = c
        know = {}
        cur = {e: {} for e in ENGS}
        waits = {}
        for o in self.ops:
            k = cur[o.eng]
            wl = []
            need = {}
            for d in o.deps:
                nm = d.sem.name
                if need.get(nm, (0, None))[0] < d.val:
                    need[nm] = (d.val, d)
            for nm, (v, d) in sorted(need.items(), key=lambda kv: kv[1][1].gidx):
                if k.get(nm, 0) >= v:
                    continue
                wl.append((d.sem, v))
                dk = know.get(d)
                if dk:
                    for a, b in dk.items():
                        if k.get(a, 0) < b:
                            k[a] = b
                if k.get(nm, 0) < v:
                    k[nm] = v
            waits[o] = wl
            if o.sig:
                snap = dict(k)
                snap[o.sem.name] = max(snap.get(o.sem.name, 0), o.val)
                know[o] = snap
        block = stack.enter_context(nc.Block())
        engmap = {"pe": block.tensor, "act": block.scalar, "dve": block.vector,
                  "pool": block.gpsimd, "sp": block.sync}
        for e in ENGS:
            ops = self.eng_ops[e]

            def body(eng, ops=ops):
                for o in ops:
                    for (s, v) in waits[o]:
                        eng.wait_ge(s, v)
                    ins = o.fn(eng)
                    if o.sig:
                        ins.then_inc(o.sem, 16 if o.dma else 1)
            engmap[e](body)


class Arena:
    def __init__(self, ap, nbytes):
        self.ap = ap
        self.nbytes = nbytes
        self.top = 0
        self.live = []
        self.ghosts = []

    def mark(self):
        return (self.top, len(self.live))

    def release(self, m):
        top, n = m
        for (a, b, rl) in self.live[n:]:
            self.ghosts.append((a, b, rl))
        del self.live[n:]
        self.top = top

    def alloc(self, nbytes, nres=1, name="", fine=True):
        a = (self.top + 31) // 32 * 32
        b = a + nbytes
        assert b <= self.nbytes, f"arena overflow {name}: {b} > {self.nbytes}"
        self.top = b
        inherit = {}
        keep = []
        for (ga, gb, rl) in self.ghosts:
            if ga < b and a < gb:
                for r in rl:
                    if r.w is not None:
                        inherit[("g", id(r), "w")] = r.w
                    for kk, vv in r.r.items():
                        inherit[("g", id(r), kk)] = vv
            keep.append((ga, gb, rl))
        res = []
        for i in range(nres):
            r = Res(f"{name}{i}")
            r.r = dict(inherit)
            res.append(r)
        if fine and nres > 1 and nbytes % nres == 0:
            sz = nbytes // nres
            for i, r in enumerate(res):
                lo, hi = a + i * sz, a + (i + 1) * sz
                inh = {}
                for (ga, gb, rl) in self.ghosts:
                    if ga < hi and lo < gb:
                        for g_ in rl:
                            if g_.w is not None:
                                inh[("g", id(g_), "w")] = g_.w
                            for kk, vv in g_.r.items():
                                inh[("g", id(g_), kk)] = vv
                r.r = inh
                self.live.append((lo, hi, [r]))
        else:
            self.live.append((a, b, res))
        return a, res

    def f32(self, n, nres=1, name=""):
        a, res = self.alloc(4 * n, nres, name)
        return self.ap[:, a // 4: a // 4 + n], res

    def bf16(self, n, nres=1, name="", fine=True):
        a, res = self.alloc(2 * n, nres, name, fine)
        return self.ap[:, a // 4: a // 4 + n // 2].bitcast(BF16), res

    def i32(self, n, nres=1, name=""):
        a, res = self.alloc(4 * n, nres, name)
        return self.ap[:, a // 4: a // 4 + n].bitcast(I32), res


ARENA_BYTES = 210944


def build_program(stop_after=99, debug=()):
    nc = bass.Bass("TRN2", target_bir_lowering=False)

    def din(name, shape, dt=F32):
        return nc.dram_tensor(name, list(shape), dt, kind="ExternalInput").ap()

    x_d = din("x", [S_LEN, D])
    mem_d = din("mem", [256, D])
    pos_d = din("posb", [128, S_LEN], I32)
    w_in_d = din("w_in", [D, 3072])
    w_o_d = din("w_o", [D, D])
    w_mq_d = din("w_mq", [D, D])
    w_mk_d = din("w_mk", [D, D])
    w_mv_d = din("w_mv", [D, D])
    w_mo_d = din("w_mo", [D, D])
    w_up_d = din("w_up", [D, 4096])
    w_dn_d = din("w_down", [4096, D])
    lam_d = din("lamb", [128, 256])
    subg_d = din("subg", [128, 1])
    bt_d = din("bt", [8, 128, 256])
    btc_d = din("btc", [128, 16])
    lnp_d = din("lnp", [3, 2, 128, D])
    cf_d = din("cf32", [128, 8])
    cb_d = din("cbf", [128, 256], BF16)
    out_d = nc.dram_tensor("out", [S_LEN, D], F32, kind="ExternalOutput").ap()

    with ExitStack() as st:
        AR = st.enter_context(nc.sbuf_tensor("arena", [128, ARENA_BYTES // 4], F32))
        PSALL = st.enter_context(nc.psum_tensor("psall", [128, 8 * 512], F32))
        PS = [PSALL[:, i * 512:(i + 1) * 512] for i in range(8)]

        def pspair(b0):
            return PSALL[:, b0 * 512:(b0 + 2) * 512].rearrange("p (t n) -> p t n", t=2)
        RPS = [Res(f"ps{i}") for i in range(8)]
        S = Sched(nc)
        A = Arena(AR, ARENA_BYTES)
        out_dmas = []

        def mm(out, lhsT, rhs, start, stop, reads, writes, skip=False):
            if skip:
                S.op("pe", lambda e: e.matmul(out, lhsT, rhs, start=start, stop=stop, skip_group_check=True),
                     reads=reads, writes=writes)
            else:
                S.op("pe", lambda e: e.matmul(out, lhsT, rhs, start=start, stop=stop), reads=reads, writes=writes)

        def tr(out, in_, reads, writes):
            S.op("pe", lambda e: e.transpose(out, in_, ident), reads=reads, writes=writes)

        def act(out, in_, func, reads, writes, scale=1.0, bias=None):
            if bias is None:
                S.op("act", lambda e: e.activation(out, in_, func, scale=scale), reads=reads, writes=writes)
            else:
                S.op("act", lambda e: e.activation(out, in_, func, bias=bias, scale=scale),
                     reads=reads, writes=writes)

        def acopy(out, in_, reads, writes):
            S.op("act", lambda e: e.copy(out, in_), reads=reads, writes=writes)

        def vcopy(out, in_, reads, writes):
            S.op("dve", lambda e: e.tensor_copy(out, in_), reads=reads, writes=writes)

        def tt(out, in0, in1, op, reads, writes):
            S.op("dve", lambda e: e.tensor_tensor(out, in0, in1, op), reads=reads, writes=writes)

        def ts(out, in0, s1, s2, op0, op1, reads, writes):
            if s2 is None:
                S.op("dve", lambda e: e.tensor_scalar(out, in0, s1, None, op0), reads=reads, writes=writes)
            else:
                S.op("dve", lambda e: e.tensor_scalar(out, in0, s1, s2, op0, op1), reads=reads, writes=writes)

        def stt(out, in0, scalar, in1, op0, op1, reads, writes):
            S.op("dve", lambda e: e.scalar_tensor_tensor(out, in0, scalar, in1, op0, op1),
                 reads=reads, writes=writes)

        def recip(out, in_, reads, writes):
            S.op("dve", lambda e: e.reciprocal(out, in_), reads=reads, writes=writes)

        def vmemset(out, val, writes):
            S.op("dve", lambda e: e.memset(out, val), writes=writes)

        def dma(q, out, in_, reads=(), writes=()):
            return S.dma(q, lambda e: e.dma_start(out=out, in_=in_), reads=reads, writes=writes)

        def dump(name, ap_sb, shape, dt, reads):
            if name not in debug:
                return
            d = nc.dram_tensor(name, list(shape), dt, kind="ExternalOutput").ap()
            out_dmas.append(dma("sp", d, ap_sb, reads=reads))

        cf, r_cf = A.f32(8, 1, "cf")
        cb, r_cb = A.bf16(256, 1, "cb")
        ones_bf, r_ob = A.bf16(128, 1, "onesb")
        ones_f, r_of = A.f32(128, 1, "onesf")
        smallf, r_small = A.f32(16, 1, "small")
        actt_flat, r_actt = A.bf16(KC * S_LEN, NT, "actt")
        ACTT = actt_flat.rearrange("p (c t) -> p c t", c=KC)
        ring = []
        for i in range(3):
            wv_, rw_ = A.bf16(4096, 1, f"ring{i}")
            ring.append((wv_, rw_[0]))
        ident = cb[:, 0:128]
        perm = cb[:, 128:256]
        invf = cf[:, 0:1]
        sgn = cf[:, 1:2]
        epsc = cf[:, 2:3]
        maskb = cf[:, 3:4]
        maskb2 = cf[:, 4:5]
        invf_lo = cf[:, 5:6]
        R_CF, R_CB, R_OB, R_OF, R_SM = r_cf[0], r_cb[0], r_ob[0], r_of[0], r_small[0]

        dma("sp", cf, cf_d, writes=[R_CF])
        dma("sp", cb, cb_d, writes=[R_CB])
        vmemset(ones_bf, 1.0, [R_OB])
        vmemset(ones_f, 1.0, [R_OF])

        pieces = []
        for off in (1024, 2560, 1536, 2048, 0, 512):
            pieces.append(("c", w_in_d, off))
        for (w_, off_) in ((w_o_d, 0), (w_o_d, 512), (w_mq_d, 0), (w_mk_d, 0), (w_mk_d, 512),
                           (w_mv_d, 0), (w_mv_d, 512), (w_mq_d, 512), (w_mo_d, 0), (w_mo_d, 512)):
            pieces.append(("c", w_, off_))
        for fg_ in range(8):
            pieces.append(("c", w_up_d, fg_ * 512))
            pieces.append(("r", w_dn_d, fg_ * 512))
        issued = [None] * len(pieces)
        piece_state = {"next": 0}

        def issue_piece(k):
            kind, w_d, off = pieces[k]
            wv_, rw_ = ring[3 if k == len(pieces) - 1 else k % 3]
            if kind == "c":
                v = wv_.rearrange("p (k n) -> p k n", k=8)
                src = w_d[:, off:off + 512].rearrange("(k p) n -> p k n", p=128)
            else:
                v = wv_.rearrange("p (k n) -> p k n", k=4)
                src = w_d[off:off + 512, :].rearrange("(k p) n -> p k n", p=128)
            dma("pool", v, src, writes=[rw_])
            issued[k] = (v, rw_)

        def next_piece(w_d, off, ahead=2):
            k = piece_state["next"]
            piece_state["next"] += 1
            assert pieces[k][1] is w_d and pieces[k][2] == off, (k, off)
            for kk in range(k, min(k + 1 + ahead, len(pieces))):
                if issued[kk] is None:
                    issue_piece(kk)
            return issued[k]

        def load_cols(w_d, c0, ahead=2):
            return next_piece(w_d, c0, ahead)

        def load_rows(w_d, r0):
            return next_piece(w_d, r0)

        bank_state = {"i": 0, "pool": list(range(8))}

        def next_bank():
            p = bank_state["pool"]
            b = p[bank_state["i"] % len(p)]
            bank_state["i"] += 1
            return b

        def set_banks(pool):
            bank_state["pool"] = list(pool)
            bank_state["i"] = 0

        evac_state = {"i": 0, "act_only": True}

        def evac_copy(out_ap, in_ap, reads, writes):
            evac_state["i"] += 1
            if evac_state["act_only"] or evac_state["i"] % 2:
                acopy(out_ap, in_ap, reads, writes)
            else:
                vcopy(out_ap, in_ap, reads, writes)

        def transpose_tile(src_bf, r_src, t):
            b = next_bank()
            pb = PS[b][:].bitcast(BF16)
            for c in range(KC):
                tr(pb[:, c * 128:(c + 1) * 128], src_bf[:, c * 128:(c + 1) * 128],
                   reads=(r_src if isinstance(r_src, list) else [r_src]) + [R_CB], writes=[RPS[b]])
            evac_copy(ACTT[:, :, t * 128:(t + 1) * 128], pb.rearrange("p (c t) -> p c t", c=KC),
                      reads=[], writes=[RPS[b], r_actt[t]])

        def gemm_feature_major(wv_, rw_, c, g, b):
            for kc in range(KC):
                mm(PS[b][:], wv_[:, kc, c * 128:(c + 1) * 128], ACTT[:, kc, g * 512:(g + 1) * 512],
                   kc == 0, kc == KC - 1, reads=[rw_] + r_actt[g * 4:(g + 1) * 4], writes=[RPS[b]])

        def gemm_token_major(wv_, rw_, t, b):
            for kc in range(KC):
                mm(PS[b][:], ACTT[:, kc, t * 128:(t + 1) * 128], wv_[:, kc, :],
                   kc == 0, kc == KC - 1, reads=[rw_, r_actt[t]], writes=[RPS[b]])

        m_mixer = A.mark()
        qkt_flat, r_qkt = A.bf16(16 * S_LEN, 16, "qkt")
        QKT = qkt_flat.rearrange("p (c t) -> p c t", c=16)
        v_flat, r_v = A.bf16(NT * 1536, NT, "v")
        V = v_flat.rearrange("p (t n) -> p t n", t=NT)
        xb = []
        for i in range(2):
            b_, rb_ = A.bf16(1024, 1, f"xb{i}")
            xb.append((b_, rb_[0]))
        m_proj = A.mark()
        if True:
            cos_t, r_cos = A.f32(S_LEN, 1, "cos")
            sin_t, r_sin = A.f32(S_LEN, 1, "sin")
            R_COS, R_SIN = r_cos[0], r_sin[0]
            scr = []
            for i in range(4):
                s_, rs_ = A.f32(512, 1, f"scr{i}")
                scr.append((s_, rs_[0]))
            q16_all, r_q16 = A.bf16(1024, 2, "q16")
            q16 = [(q16_all[:, 0:512], r_q16[0]), (q16_all[:, 512:1024], r_q16[1])]
            xb.append((scr[1][0].bitcast(BF16), [scr[1][1]]))
            xb.append((q16_all, list(r_q16)))
            posi, r_posi = A.i32(512, 1, "posi")
            vones = V[:, :, 512:1536].rearrange("p t (h e) -> p t h e", h=8)[:, :, :, 64:128]
            vmemset(vones, 1.0, r_v)
            two_pi = 2.0 * math.pi
            pf, rpf = scr[0]
            kf, rkf = scr[2]
            ki = scr[3][0].bitcast(I32)
            rki = scr[3][1]
            for g in range(NG):
                cs = slice(g * 512, (g + 1) * 512)
                dma("sp", posi, pos_d[:, cs], writes=[r_posi[0]])
                vcopy(pf, posi, [r_posi[0]], [rpf])
                ts(kf, pf, invf_lo, None, ALU.mult, None, [R_CF, rpf], [rkf])
                stt(pf, pf, invf, kf, ALU.mult, ALU.add, [R_CF, rkf], [rpf])
                for which, tab, rtab in ((0, sin_t, R_SIN), (1, cos_t, R_COS)):
                    shift = 0.0 if which == 0 else math.pi / 2
                    ang = tab[:, cs]
                    ts(ang, pf, shift, None, ALU.add, None, [rpf], [rtab])
                    ts(kf, ang, 1.0 / two_pi, None, ALU.mult, None, [rtab], [rkf])
                    vcopy(ki, kf, [rkf], [rki])
                    vcopy(kf, ki, [rki], [rkf])
                    stt(ang, kf, -CW1, ang, ALU.mult, ALU.add, [rkf], [rtab])
                    stt(ang, kf, -CW2, ang, ALU.mult, ALU.add, [rkf], [rtab])
                    ts(kf, ang, math.pi, -two_pi, ALU.is_gt, ALU.mult, [rtab], [rkf])
                    tt(ang, ang, kf, ALU.add, [rkf], [rtab])
                    ts(kf, ang, -math.pi, two_pi, ALU.is_lt, ALU.mult, [rtab], [rkf])
                    tt(ang, ang, kf, ALU.add, [rkf], [rtab])
            for t in range(NT):
                b_, rb_ = xb[t % 4]
                rb_ = rb_ if isinstance(rb_, list) else [rb_]
                dma("pool", b_, x_d[t * 128:(t + 1) * 128, :], writes=rb_)
                if t == 0:
                    piece_state["next"] = 1
                if t == 1:
                    issue_piece(0)
                    wva, rwva = issued[0]
                if t == 8:
                    issue_piece(1)
                if t == 12:
                    issue_piece(2)
                transpose_tile(b_, rb_, t)
                for tv in ([] if t == 0 else ([0, 1] if t == 1 else [t])):
                    bq = next_bank()
                    gemm_token_major(wva, rwva, tv, bq)
                    evac_copy(V[:, tv, 0:512], PS[bq][:], [], [RPS[bq], r_v[tv]])
            dump("dbg_xt", actt_flat, [128, KC * S_LEN], BF16, r_actt)

            rope_cnt = {"i": 0}

            def rope_finish(b, g, dst, rdst, qb_, rqb_):
                b2 = next_bank()
                mm(PS[b2][:], perm, qb_, True, True, reads=[rqb_, R_CB], writes=[RPS[b2]])
                k2 = 2 * (rope_cnt["i"] % 2)
                rope_cnt["i"] += 1
                t1, rt1 = scr[k2]
                t2, rt2 = scr[k2 + 1]
                cs = slice(g * 512, (g + 1) * 512)
                tt(t1, PS[b][:], cos_t[:, cs], ALU.mult, [R_COS], [RPS[b], rt1])
                tt(t2, PS[b2][:], sin_t[:, cs], ALU.mult, [R_SIN], [RPS[b2], rt2])
                S.op("pool", lambda e: e.tensor_tensor(dst, t1, t2, ALU.add), reads=[rt1, rt2], writes=[rdst])

            def proj_feature_major(w_d, c0, chunk0, rope):
                wv_, rw_ = load_cols(w_d, c0)
                pend = None
                n = 0
                for c in range(4):
                    for g in range(NG):
                        b = next_bank()
                        gemm_feature_major(wv_, rw_, c, g, b)
                        dst = QKT[:, chunk0 + c, g * 512:(g + 1) * 512]
                        rdst = r_qkt[chunk0 + c]
                        if not rope:
                            evac_copy(dst, PS[b][:], [], [RPS[b], rdst])
                            continue
                        if pend is not None:
                            rope_finish(*pend)
                        qb_, rqb_ = q16[n % 2]
                        n += 1
                        acopy(qb_, PS[b][:], [], [RPS[b], rqb_])
                        pend = (b, g, dst, rdst, qb_, rqb_)
                if pend is not None:
                    rope_finish(*pend)

            def proj_token_major_v(w_d, c0, ext):
                wv_, rw_ = load_cols(w_d, c0)
                for t in range(NT):
                    b = next_bank()
                    gemm_token_major(wv_, rw_, t, b)
                    if not ext:
                        evac_copy(V[:, t, 0:512], PS[b][:], [], [RPS[b], r_v[t]])
                    else:
                        dst = V[:, t, 512:1536].rearrange("p (h e) -> p h e", h=8)[:, :, 0:64]
                        src = PS[b][:].rearrange("p (h e) -> p h e", h=8)
                        evac_copy(dst, src, [], [RPS[b], r_v[t]])

            proj_token_major_v(w_in_d, 2560, True)
            proj_feature_major(w_in_d, 1536, 8, False)
            act(sin_t, sin_t, AF.Sin, [], [R_SIN])
            act(cos_t, cos_t, AF.Sin, [], [R_COS])
            ts(sin_t, sin_t, sgn, None, ALU.mult, None, [R_CF], [R_SIN])
            proj_feature_major(w_in_d, 2048, 12, False)
            evac_state["act_only"] = False
            proj_feature_major(w_in_d, 0, 0, True)
            proj_feature_major(w_in_d, 512, 4, True)
            dump("dbg_qkt", qkt_flat, [128, 16 * S_LEN], BF16, r_qkt)
            dump("dbg_v", v_flat, [128, NT * 1536], BF16, r_v)
        A.release(m_proj)

        if stop_after >= 2:
            fs = []
            for i in range(7):
                f_, rf_ = A.f32(512, 1, f"fs{i}")
                fs.append((f_, rf_[0]))
            bt_flat, r_bt = A.f32(8 * 256, 1, "bt")
            BT = bt_flat.rearrange("p (h j) -> p h j", h=8)
            R_BT = r_bt[0]
            pts = []
            ptp = []

            def alloc_ptp(i):
                p_, rp_ = A.bf16(1024, 1, f"ptp{i}")
                ptp.append((p_.rearrange("p (t n) -> p t n", t=2), rp_[0]))
                pts.append((p_[:, 0:512], rp_[0]))
                pts.append((p_[:, 512:1024], rp_[0]))
            alloc_ptp(0)
            alloc_ptp(1)
            btc_all, r_btc = A.f32(16, 1, "btc")
            btc = btc_all[:, 0:8]
            btc2 = btc_all[:, 8:16]

            def load_bias_tables():
                dma("sp", BT, bt_d.rearrange("h p j -> p h j"), writes=[R_BT])
                dma("sp", btc_all, btc_d, writes=[R_BT])
                for h_ in range(8):
                    ts(BT[:, h_, :], BT[:, h_, :], btc[:, h_:h_ + 1], None, ALU.subtract, None, [], [R_BT])
                ts(bt_flat, bt_flat, 8.0, None, ALU.mult, None, [], [R_BT])
            lamt, r_lam = A.f32(256, 1, "lam")
            dma("sp", lamt, lam_d, writes=[r_lam[0]])
            gcol, r_g = A.f32(1, 1, "gcol")
            R_G = r_g[0]
            dma("sp", gcol, subg_d, writes=[R_G])
            lp, r_lp = A.f32(128, 1, "lp")
            neglam = smallf[:, 0:1]
            e1 = smallf[:, 1:2]
            e2 = smallf[:, 2:3]
            lp2 = lp.rearrange("p (a d) -> p a d", a=2)
            l4 = lamt.rearrange("p (a d) -> p a d", a=4)
            tt(lp2[:, 0, :], l4[:, 0, :], l4[:, 1, :], ALU.mult, [r_lam[0]], [r_lp[0]])
            tt(lp2[:, 1, :], l4[:, 2, :], l4[:, 3, :], ALU.mult, [r_lam[0]], [r_lp[0]])
            S.op("dve", lambda e: e.reduce_sum(e1, lp2[:, 0, :], mybir.AxisListType.X),
                 reads=[r_lp[0]], writes=[R_SM])
            S.op("dve", lambda e: e.reduce_sum(e2, lp2[:, 1, :], mybir.AxisListType.X),
                 reads=[r_lp[0]], writes=[R_SM])
            act(e1, e1, AF.Exp, [], [R_SM])
            act(e2, e2, AF.Exp, [], [R_SM])
            stt(neglam, e2, -LAMBDA_INIT, e1, ALU.add, ALU.subtract, [], [R_SM])
            ts(gcol, gcol, 1.0 - LAMBDA_INIT, None, ALU.mult, None, [], [R_G])
            alloc_ptp(2)

            def diff_qk_burst(h, G, kt):
                q0 = G * 512
                r = kt - 4 * G
                a = 0 if r < 0 else 128 * r
                bs = (4 + 2 * (kt % 2), 5 + 2 * (kt % 2))
                for t in range(2):
                    rows = slice(t * 64, (t + 1) * 64)
                    mm(PS[bs[t]][:, a:512], QKT[rows, 4 + h, kt * 128:(kt + 1) * 128],
                       QKT[rows, h, q0 + a:q0 + 512], True, True, reads=[r_qkt[4 + h], r_qkt[h]],
                       writes=[RPS[bs[0]], RPS[bs[1]]] if t == 0 else [RPS[bs[1]]])

            def diff_exps(h, G, kt):
                r = kt - 4 * G
                sl = kt % 2
                pp = pspair(4 + 2 * sl)
                pt2, rpt = ptp[sl]
                wr = [RPS[4 + 2 * sl], RPS[5 + 2 * sl], rpt]
                if r < 0:
                    act(pt2, pp, AF.Exp, [], wr, scale=0.125)
                    return
                a = 128 * r
                act(pt2[:, :, a:a + 64], pp[:, :, a:a + 64], AF.Exp, [R_CF], wr, scale=0.125, bias=maskb)
                if a + 64 < 512:
                    act(pt2[:, :, a + 64:512], pp[:, :, a + 64:512], AF.Exp, [], wr, scale=0.125)

            def diff_av_burst(h, G, kt, first):
                r = kt - 4 * G
                a = 0 if r < 0 else 128 * r
                rp = [pts[2 * (kt % 2)][1], pts[2 * (kt % 2) + 1][1]]
                n = 0
                for t in range(2):
                    pt, rpt = pts[2 * (kt % 2) + t]
                    for (bank, lhs) in ((t, V[:, kt, h * 128:(h + 1) * 128]), (2 + t, ones_bf)):
                        st_ = first[bank]
                        first[bank] = False
                        mm(PS[bank][:, a:512], lhs, pt[:, a:512], st_, False,
                           reads=(rp if n == 0 else [rpt]) + [r_v[kt], R_OB],
                           writes=[RPS[0], RPS[1], RPS[2], RPS[3]] if (n == 0 and not st_) else [RPS[bank]],
                           skip=True)
                        n += 1

            def diff_epilogue1(h, G, par):
                lz0, rlz0 = fs[0]
                lz1, rlz1 = fs[1]
                o0, ro0 = fs[2 + 4 * par]
                o1, ro1 = fs[3]
                act(lz0, PS[2][:], AF.Ln, [], [RPS[2], rlz0])
                act(lz1, PS[3][:], AF.Ln, [], [RPS[3], rlz1])
                vcopy(o0, PS[0][:], [], [RPS[0], ro0])
                vcopy(o1, PS[1][:], [], [RPS[1], ro1])

            def diff_epilogue2a(h, G, par):
                lz0, rlz0 = fs[0]
                lz1, rlz1 = fs[1]
                o0, ro0 = fs[2 + 4 * par]
                o1, ro1 = fs[3]
                osq, rosq = fs[4]
                osq_b = osq.bitcast(BF16)[:, 0:512]
                act(lz0, lz0, AF.Exp, [], [rlz0], scale=-1.0)
                act(lz1, lz1, AF.Exp, [], [rlz1], scale=-1.0)
                tt(o0, o0, lz0, ALU.mult, [rlz0], [ro0])
                tt(o1, o1, lz1, ALU.mult, [rlz1], [ro1])
                stt(o0, o1, neglam, o0, ALU.mult, ALU.add, [ro1, R_SM], [ro0])
                tt(osq_b, o0, o0, ALU.mult, [ro0], [rosq])

            def diff_epilogue2b(h, G, par, bk):
                q0 = G * 512
                rs_, rrs_ = fs[5]
                o0, ro0 = fs[2 + 4 * par]
                osq, rosq = fs[4]
                osq_b = osq.bitcast(BF16)[:, 0:512]
                mm(PS[bk][:], ones_bf, osq_b, True, True, reads=[rosq, R_OB], writes=[RPS[bk]])
                act(rs_, PS[bk][:], AF.Ln, [R_CF], [RPS[bk], rrs_], scale=1.0 / 128.0, bias=epsc)
                act(rs_, rs_, AF.Exp, [], [rrs_], scale=-0.5)
                stt(ACTT[:, h, q0:q0 + 512], o0, gcol, rs_, ALU.mult, ALU.mult, [ro0, rrs_, R_G],
                    r_actt[G * 4:(G + 1) * 4])

            deferred = None
            dcount = 0
            for h in range(4):
                for G in range(NG):
                    first = {0: True, 1: True, 2: True, 3: True}
                    nkt = 4 * G + 4
                    diff_qk_burst(h, G, 0)
                    for kt in range(nkt):
                        if kt + 1 < nkt:
                            diff_qk_burst(h, G, kt + 1)
                        diff_exps(h, G, kt)
                        if deferred is not None:
                            if kt == 1:
                                diff_epilogue2a(*deferred)
                            if kt == min(5, nkt - 1):
                                diff_epilogue2b(*deferred, 4 + 2 * (kt % 2))
                                deferred = None
                        diff_av_burst(h, G, kt, first)
                    diff_epilogue1(h, G, dcount % 2)
                    deferred = (h, G, dcount % 2)
                    dcount += 1
                    if dcount == 2:
                        load_bias_tables()
            diff_epilogue2a(*deferred)
            diff_epilogue2b(*deferred, 6)

            for t_ in range(8):
                dma("sp", qkt_flat[:, t_ * 2048:(t_ + 1) * 2048].bitcast(F32), x_d[t_ * 128:(t_ + 1) * 128, :],
                    writes=[r_qkt[t_]])

            def chunk_geom(G, kt):
                q0 = G * 512
                return max(0, q0 - 128 * kt), min(640, q0 + 512 - 128 * kt)

            def chunk_qk_pair(hp, G, kt, slot):
                j0, j1 = chunk_geom(G, kt)
                bs = (2 + 2 * slot, 3 + 2 * slot)
                if WARM_FILL:
                    mm(PS[6][:], ones_bf, V[:, kt, 0:512], True, True, reads=[R_OB, r_v[kt]], writes=[RPS[6]])
                for i in range(2):
                    rows = slice(i * 64, i * 64 + 64)
                    mm(PS[bs[i]][:, 0:j1 - j0], QKT[rows, 12 + hp, kt * 128:(kt + 1) * 128],
                       QKT[rows, 8 + hp, 128 * kt + j0:128 * kt + j1], True, True,
                       reads=[r_qkt[12 + hp], r_qkt[8 + hp]],
                       writes=[RPS[bs[0]], RPS[bs[1]]] if i == 0 else [RPS[bs[1]]])

            def chunk_exp_pair(hp, G, kt, slot):
                j0, j1 = chunk_geom(G, kt)
                w = j1 - j0
                wa = max(0, min(j1, 256) - j0)
                wm = min(j1, 576) - j0
                pp = pspair(2 + 2 * slot)
                pt2, rpt = ptp[slot]
                wr = [RPS[2 + 2 * slot], RPS[3 + 2 * slot]]
                if wa > 0:
                    tt(pp[:, :, 0:wa], pp[:, :, 0:wa], BT[:, 2 * hp:2 * hp + 2, j0:j0 + wa], ALU.add, [R_BT], wr)
                act(pt2[:, :, 0:wm], pp[:, :, 0:wm], AF.Exp, [], wr + [rpt], scale=0.125)
                if wm < w:
                    act(pt2[:, :, wm:w], pp[:, :, wm:w], AF.Exp, [R_CF], wr + [rpt], scale=0.125, bias=maskb2)

            def chunk_av_pair(hp, G, kt, slot, first):
                q0 = G * 512
                j0, j1 = chunk_geom(G, kt)
                c0 = 128 * kt + j0 - q0
                pt2, rpt = ptp[slot]
                for i in range(2):
                    h = 2 * hp + i
                    st_ = first[i]
                    first[i] = False
                    mm(PS[i][:, c0:c0 + j1 - j0], V[:, kt, 512 + h * 128:512 + (h + 1) * 128],
                       pt2[:, i, 0:j1 - j0], st_, False, reads=[rpt, r_v[kt]],
                       writes=[RPS[0], RPS[1]] if i == 0 else [RPS[i]], skip=True)

            def chunk_epilogue_pair(hp, G, par):
                q0 = G * 512
                for i in range(2):
                    rows = slice(i * 64, i * 64 + 64)
                    rz, rrz = fs[2 * par + i]
                    act(rz[0:64, :], PS[i][64:128, :], AF.Ln, [], [RPS[i], rrz])
                    act(rz[0:64, :], rz[0:64, :], AF.Exp, [], [rrz], scale=-1.0)
                    tt(ACTT[rows, 4 + hp, q0:q0 + 512], PS[i][0:64, :], rz[0:64, :], ALU.mult,
                       [rrz], [RPS[i]] + r_actt[G * 4:(G + 1) * 4])

            WARM_FILL = False
            NSL = 2 if WARM_FILL else 3
            gcount = 0
            bcount = 0
            for hp in range(4):
                for G in range(NG):
                    kts = list(range(max(0, 4 * G - 4), 4 * G + 4))
                    first = [True, True]
                    n = len(kts)
                    slots = [(bcount + i) % NSL for i in range(n)]
                    bcount += n
                    LA = NSL - 1
                    for i in range(min(LA, n)):
                        chunk_qk_pair(hp, G, kts[i], slots[i])
                    for i, kt in enumerate(kts):
                        if i + LA < n:
                            chunk_qk_pair(hp, G, kts[i + LA], slots[i + LA])
                        chunk_exp_pair(hp, G, kt, slots[i])
                        chunk_av_pair(hp, G, kt, slots[i], first)
                    chunk_epilogue_pair(hp, G, gcount % 2)
                    gcount += 1
            set_banks(range(8))
            dump("dbg_yt", actt_flat, [128, KC * S_LEN], BF16, r_actt)
        A.release(m_mixer)

        if stop_after >= 3:
            x_flat, r_x = A.f32(NT * D, NT, "X")
            X = x_flat.rearrange("p (t n) -> p t n", t=NT)
            gb_flat, r_gb = A.f32(2 * D, 1, "gb")
            GB = gb_flat.rearrange("p (a n) -> p a n", a=2)
            R_GB = r_gb[0]
            xs16 = []
            for i in range(2):
                b_, rb_ = A.bf16(D, 1, f"xs16{i}")
                xs16.append((b_, rb_[0]))
            stats_t, r_stats = A.f32(64, 3, "stats")
            lnc = {"i": 0}

            NSLOT = 3
            SW = 20

            def ln_s1(t):
                i = lnc["i"] % NSLOT
                lnc["i"] += 1
                rs = r_stats[i]
                st6 = stats_t[:, i * SW:i * SW + 12].rearrange("p (a b) -> p a b", a=2)
                mv = stats_t[:, i * SW + 12:i * SW + 14]
                negmean = stats_t[:, i * SW + 15:i * SW + 16]
                xt = X[:, t, :]
                S.op("dve", lambda e: e.bn_stats(st6[:, 0, :], xt[:, 0:512]), reads=[r_x[t]], writes=[rs])
                S.op("dve", lambda e: e.bn_stats(st6[:, 1, :], xt[:, 512:1024]), reads=[r_x[t]], writes=[rs])
                S.op("dve", lambda e: e.bn_aggr(mv, st6), reads=[], writes=[rs])
                ts(negmean, mv[:, 0:1], -1.0, None, ALU.mult, None, [], [rs])
                return i

            def ln_s2(t, i):
                rs = r_stats[i]
                mv = stats_t[:, i * SW + 12:i * SW + 14]
                rstd = stats_t[:, i * SW + 14:i * SW + 15]
                negmean = stats_t[:, i * SW + 15:i * SW + 16]
                nmr = stats_t[:, i * SW + 16:i * SW + 17]
                xt = X[:, t, :]
                act(rstd, mv[:, 1:2], AF.Ln, [R_CF], [rs], bias=epsc)
                act(rstd, rstd, AF.Exp, [], [rs], scale=-0.5)
                act(nmr, negmean, AF.Identity, [], [rs], scale=rstd)
                act(xt, xt, AF.Identity, [rs], [r_x[t]], scale=rstd, bias=nmr)

            def ln_gb(t, out_dst=None):
                xt = X[:, t, :]
                if out_dst is not None and t % 2 == 1:
                    S.op("pool", lambda e: e.tensor_tensor(xt, xt, GB[:, 0, :], ALU.mult), reads=[R_GB], writes=[r_x[t]])
                else:
                    tt(xt, xt, GB[:, 0, :], ALU.mult, [R_GB], [r_x[t]])
                S.op("pool", lambda e: e.tensor_tensor(xt, xt, GB[:, 1, :], ALU.add), reads=[R_GB], writes=[r_x[t]])
                if out_dst is not None:
                    out_dmas.append(dma("sp", out_dst, xt, reads=[r_x[t]]))

            def ln_cast(t):
                xb_, rxb_ = xs16[t % 2]
                acopy(xb_, X[:, t, :], [r_x[t]], [rxb_])

            def ln_tr(t):
                xb_, rxb_ = xs16[t % 2]
                prev = evac_state["act_only"]
                evac_state["act_only"] = True
                transpose_tile(xb_, rxb_, t)
                evac_state["act_only"] = prev

            class LNPipe:
                def __init__(self, to_actt, dst_fn=None):
                    self.to_actt = to_actt
                    self.dst_fn = dst_fn
                    self.q = []

                def _advance(self):
                    keep = []
                    for ent in self.q:
                        t, i, stg = ent
                        if stg == 0:
                            ln_s2(t, i)
                        elif stg == 1:
                            ln_gb(t, None if self.dst_fn is None else self.dst_fn(t))
                        elif stg == 2:
                            ln_cast(t)
                        elif stg == 3:
                            ln_tr(t)
                        ent[2] += 1
                        if ent[2] < (4 if self.to_actt else 2):
                            keep.append(ent)
                    self.q = keep

                def push(self, t):
                    i = ln_s1(t)
                    self._advance()
                    self.q.append([t, i, 0])

                def flush(self, fillers=()):
                    fillers = list(fillers)
                    while self.q:
                        self._advance()
                        if fillers:
                            fillers.pop(0)()
                    for f_ in fillers:
                        f_()

            def load_ln(idx):
                dma("sp", GB, lnp_d[idx].rearrange("a p n -> p a n"), writes=[R_GB])

            def out_proj(w_d, ln_idx, first_from_hbm, after_gemms=None, at_tile=None, drain=()):
                load_ln(ln_idx)
                xin = []
                if first_from_hbm:
                    for i in range(3):
                        b_, rb_ = A.f32(512, 1, f"xin{i}")
                        xin.append((b_, rb_[0]))
                cnt = 0
                pipe = LNPipe(True)
                ws = [load_cols(w_d, 0), load_cols(w_d, 512, ahead=1)]
                for t in range(NT):
                    for j in range(2):
                        wv_, rw_ = ws[j]
                        b = next_bank()
                        gemm_token_major(wv_, rw_, t, b)
                        dst = X[:, t, j * 512:(j + 1) * 512]
                        if first_from_hbm and t >= 8:
                            xi, rxi = xin[cnt % 3]
                            cnt += 1
                            dma("sp", xi, x_d[t * 128:(t + 1) * 128, j * 512:(j + 1) * 512], writes=[rxi])
                            stt(dst, xi, ALPHA, PS[b][:], ALU.mult, ALU.add, [rxi], [RPS[b], r_x[t]])
                        else:
                            stt(dst, dst, ALPHA, PS[b][:], ALU.mult, ALU.add, [], [RPS[b], r_x[t]])
                    pipe.push(t)
                    if at_tile is not None and t in at_tile:
                        at_tile[t]()
                if after_gemms is not None:
                    after_gemms()
                pipe.flush(drain)

            memt_flat, r_memt = A.bf16(KC * 256, 1, "memt")
            MEMT = memt_flat.rearrange("p (c t) -> p c t", c=KC)
            mb_flat, r_mb = A.bf16(2 * D, 1, "memb")
            MB = mb_flat.rearrange("p (t n) -> p t n", t=2)
            R_MEMT, R_MB = r_memt[0], r_mb[0]
            dma("pool", MB, mem_d.rearrange("(t p) n -> p t n", p=128), writes=[R_MB])

            def mem_transposes():
                for mt in range(2):
                    b = next_bank()
                    pb = PS[b][:].bitcast(BF16)
                    for c in range(KC):
                        tr(pb[:, c * 128:(c + 1) * 128], MB[:, mt, c * 128:(c + 1) * 128],
                           reads=[R_MB, R_CB], writes=[RPS[b]])
                    evac_copy(MEMT[:, :, mt * 128:(mt + 1) * 128], pb.rearrange("p (c t) -> p c t", c=KC),
                              [], [RPS[b], R_MEMT])
            m4 = A.mark()
            qmt_flat, r_qmt = A.bf16(KC * S_LEN, KC, "qmt")
            QMT = qmt_flat.rearrange("p (c t) -> p c t", c=KC)
            kmt_flat, r_kmt = A.bf16(KC * 256, 1, "kmt")
            KMT = kmt_flat.rearrange("p (c t) -> p c t", c=KC)
            vm_flat, r_vm = A.bf16(2 * D, 1, "vm")
            VM = vm_flat.rearrange("p (t n) -> p t n", t=2)
            R_KMT, R_VM = r_kmt[0], r_vm[0]

            qstate = {}

            def q_units(j, g):
                wv_, rw_ = qstate[j]
                for c in range(4):
                    b = next_bank()
                    gemm_feature_major(wv_, rw_, c, g, b)
                    evac_copy(QMT[:, j * 4 + c, g * 512:(g + 1) * 512], PS[b][:], [],
                              [RPS[b], r_qmt[j * 4 + c]])

            def q_fill0():
                qstate[0] = next_piece(w_mq_d, 0, ahead=0)
                q_units(0, 0)

            def k_proj(j):
                wv_, rw_ = next_piece(w_mk_d, j * 512, ahead=1 - j)
                for c in range(4):
                    b = next_bank()
                    for kc in range(KC):
                        mm(PS[b][:, 0:256], wv_[:, kc, c * 128:(c + 1) * 128], MEMT[:, kc, :],
                           kc == 0, kc == KC - 1, reads=[rw_, R_MEMT], writes=[RPS[b]])
                    evac_copy(KMT[:, j * 4 + c, :], PS[b][:, 0:256], [], [RPS[b], R_KMT])

            m3 = A.mark()
            out_proj(w_o_d, 0, True, after_gemms=lambda: (issue_piece(9), issue_piece(10)),
                     at_tile={6: mem_transposes, 8: q_fill0, 12: lambda: q_units(0, 1)},
                     drain=[lambda: q_units(0, 2), lambda: k_proj(0), lambda: k_proj(1)])
            A.release(m3)
            dump("dbg_x1", x_flat, [128, NT * D], F32, r_x)

        if stop_after >= 4:
            q_units(0, 3)
            for j in range(2):
                wv_, rw_ = load_cols(w_mv_d, j * 512)
                for mt in range(2):
                    b = next_bank()
                    for kc in range(KC):
                        mm(PS[b][:], MEMT[:, kc, mt * 128:(mt + 1) * 128], wv_[:, kc, :],
                           kc == 0, kc == KC - 1, reads=[rw_, R_MEMT], writes=[RPS[b]])
                    evac_copy(VM[:, mt, j * 512:(j + 1) * 512], PS[b][:], [], [RPS[b], R_VM])
            qstate[1] = load_cols(w_mq_d, 512)
            for g in range(NG):
                q_units(1, g)
            ptm = []
            for i in range(4):
                p_, rp_ = A.bf16(512, 1, f"ptm{i}")
                ptm.append((p_, rp_[0]))
            rzs = []
            for i in range(2):
                f_, rf_ = A.f32(512, 1, f"rzm{i}")
                rzs.append((f_, rf_[0]))
            def cross_scores(h, G, idx):
                cs = slice(G * 512, (G + 1) * 512)
                pp = []
                for mt in range(2):
                    b = next_bank()
                    for dc in range(2):
                        mm(PS[b][:], KMT[:, 2 * h + dc, mt * 128:(mt + 1) * 128], QMT[:, 2 * h + dc, cs],
                           dc == 0, dc == 1, reads=[R_KMT, r_qmt[2 * h + dc]], writes=[RPS[b]])
                    pt, rpt = ptm[(2 * idx + mt) % 4]
                    act(pt, PS[b][:], AF.Exp, [], [RPS[b], rpt], scale=1.0 / 16.0)
                    pp.append((pt, rpt))
                return pp

            def cross_values(h, G, idx, pp):
                cs = slice(G * 512, (G + 1) * 512)
                bz = next_bank()
                for mt in range(2):
                    mm(PS[bz][:], ones_bf, pp[mt][0], mt == 0, mt == 1,
                       reads=[pp[mt][1], R_OB], writes=[RPS[bz]])
                rz, rrz = rzs[idx % 2]
                act(rz, PS[bz][:], AF.Ln, [], [RPS[bz], rrz])
                act(rz, rz, AF.Exp, [], [rrz], scale=-1.0)
                for ec in range(2):
                    bo = next_bank()
                    for mt in range(2):
                        mm(PS[bo][:], VM[:, mt, h * 256 + ec * 128:h * 256 + (ec + 1) * 128], pp[mt][0],
                           mt == 0, mt == 1, reads=[pp[mt][1], R_VM], writes=[RPS[bo]])
                    tt(ACTT[:, 2 * h + ec, cs], PS[bo][:], rz, ALU.mult, [rrz],
                       [RPS[bo]] + r_actt[G * 4:(G + 1) * 4])

            groups = [(h, G) for h in range(4) for G in range(NG)]
            pend = cross_scores(groups[0][0], groups[0][1], 0)
            for i, (h, G) in enumerate(groups):
                nxt = None
                if i + 1 < len(groups):
                    nxt = cross_scores(groups[i + 1][0], groups[i + 1][1], i + 1)
                cross_values(h, G, i, pend)
                pend = nxt
            dump("dbg_ot", actt_flat, [128, KC * S_LEN], BF16, r_actt)
            A.release(m4)
            hts = []
            for i in range(2):
                h_, rh_ = A.bf16(4 * S_LEN, NT, f"ht{i}", fine=False)
                hts.append((h_.rearrange("p (c t) -> p c t", c=4), rh_))
            rl = []
            for i in range(3):
                f_, rf_ = A.f32(512, 1, f"relu{i}")
                rl.append((f_, rf_[0]))
            rcs = {"i": 0}
            up0 = {}

            def mlp_up(fg, wv_, rw_, HT, r_ht, g):
                for fc in range(4):
                    b = next_bank()
                    gemm_feature_major(wv_, rw_, fc, g, b)
                    r_, rr_ = rl[rcs["i"] % 3]
                    rcs["i"] += 1
                    act(r_, PS[b][:], AF.Relu, [], [RPS[b], rr_])
                    tt(HT[:, fc, g * 512:(g + 1) * 512], r_, r_, ALU.mult, [rr_], r_ht[g * 4:(g + 1) * 4])

            def mlp_down(fg, wd, rwd, HT, r_ht, t):
                for half in range(2):
                    b = next_bank()
                    for fc in range(4):
                        mm(PS[b][:], HT[:, fc, t * 128:(t + 1) * 128], wd[:, fc, half * 512:(half + 1) * 512],
                           fc == 0, fc == 3, reads=[rwd, r_ht[t]], writes=[RPS[b]])
                    dst = X[:, t, half * 512:(half + 1) * 512]
                    if fg == 0:
                        stt(dst, dst, ALPHA, PS[b][:], ALU.mult, ALU.add, [], [RPS[b], r_x[t]])
                    else:
                        tt(dst, dst, PS[b][:], ALU.add, [], [RPS[b], r_x[t]])
                if fg == 7:
                    pipe3.push(t)

            def up_fill(g):
                if "w" not in up0:
                    up0["w"] = next_piece(w_up_d, 0, ahead=0)
                mlp_up(0, up0["w"][0], up0["w"][1], hts[0][0], hts[0][1], g)

            def down_fill(t0, t1):
                if "d" not in up0:
                    up0["d"] = next_piece(w_dn_d, 0, ahead=1)
                for t in range(t0, t1):
                    mlp_down(0, up0["d"][0], up0["d"][1], hts[0][0], hts[0][1], t)

            out_proj(w_mo_d, 1, False, after_gemms=lambda: (issue_piece(17), issue_piece(18)),
                     at_tile={8: lambda: up_fill(0), 12: lambda: up_fill(1)},
                     drain=[lambda: up_fill(2), lambda: down_fill(0, 4), lambda: down_fill(4, 8)])
            dump("dbg_x2", x_flat, [128, NT * D], F32, r_x)

        if stop_after >= 6:
            load_ln(2)
            pipe3 = LNPipe(False, lambda t_: out_d[t_ * 128:(t_ + 1) * 128, :])
            x3_, rx3_ = A.bf16(4096, 1, "ring3")
            ring.append((x3_, rx3_[0]))

            def mlp_down2(wds, HTs, t):
                for half in range(2):
                    b = next_bank()
                    n = 0
                    for (wd, rwd), (HT, r_ht) in zip(wds, HTs):
                        for fc in range(4):
                            mm(PS[b][:], HT[:, fc, t * 128:(t + 1) * 128], wd[:, fc, half * 512:(half + 1) * 512],
                               n == 0, n == 7, reads=[rwd, r_ht[t]], writes=[RPS[b]])
                            n += 1
                    dst = X[:, t, half * 512:(half + 1) * 512]
                    tt(dst, dst, PS[b][:], ALU.add, [], [RPS[b], r_x[t]])
                pipe3.push(t)

            up_fill(3)
            down_fill(8, NT)
            for fg in range(1, 6):
                HT, r_ht = hts[fg % 2]
                wv_, rw_ = load_cols(w_up_d, fg * 512)
                for g in range(NG):
                    mlp_up(fg, wv_, rw_, HT, r_ht, g)
                wd, rwd = load_rows(w_dn_d, fg * 512)
                for t in range(NT):
                    mlp_down(fg, wd, rwd, HT, r_ht, t)
            wu6 = load_cols(w_up_d, 6 * 512)
            wd6 = load_rows(w_dn_d, 6 * 512)
            wu7 = load_cols(w_up_d, 7 * 512)
            wd7 = load_rows(w_dn_d, 7 * 512)
            mlp_up(6, wu6[0], wu6[1], hts[0][0], hts[0][1], 0)
            mlp_up(7, wu7[0], wu7[1], hts[1][0], hts[1][1], 0)
            for g in range(NG):
                if g + 1 < NG:
                    mlp_up(6, wu6[0], wu6[1], hts[0][0], hts[0][1], g + 1)
                    mlp_up(7, wu7[0], wu7[1], hts[1][0], hts[1][1], g + 1)
                for t in range(4 * g, 4 * g + 4):
                    mlp_down2((wd6, wd7), (hts[0], hts[1]), t)
            pipe3.flush()

        if stop_after < 6:
            z_, rz_ = A.f32(D, 1, "zero")
            vmemset(z_, 0.0, [rz_[0]])
            for t in range(NT):
                out_dmas.append(dma("sp", out_d[t * 128:(t + 1) * 128, :], z_, reads=[rz_[0]]))
        S.fence("sp", out_dmas)
        S.emit(st)
    return nc


def _host_constants():
    p = np.arange(128)
    j = (p % 64) % 32
    invf64 = 1.0 / (10000.0 ** (np.arange(0, 64, 2, dtype=np.float64) / 64.0))
    invf = invf64.astype(np.float32)
    invf_lo = (invf64 - invf.astype(np.float64)).astype(np.float32)
    cf = np.zeros((128, 8), np.float32)
    cf[:, 0] = invf[j]
    cf[:, 5] = invf_lo[j]
    cf[:, 1] = np.where((p % 64) < 32, -1.0, 1.0)
    cf[:, 2] = LN_EPS
    cf[:, 3] = np.where(p < 64, 0.0, MASK_NEG)
    cf[:, 4] = np.where(p < 64, MASK_NEG, 0.0)
    ident = np.eye(128, dtype=np.float32)
    partner = np.where((p % 64) < 32, p + 32, p - 32)
    perm = np.zeros((128, 128), np.float32)
    perm[partner, p] = 1.0
    cb = np.concatenate([ident, perm], axis=1).astype(ml_dtypes.bfloat16)
    i = np.arange(128)[:, None]
    jj = np.arange(256)[None, :]
    rel_idx = np.clip(jj - i, -128, 128) + 128
    return cf, cb, rel_idx


def _prep(x, mem, positions, w_in, diff_lambda, subln_g, rel_bias, w_o, ln1_g, ln1_b,
          w_mq, w_mk, w_mv, w_mo, ln2_g, ln2_b, w_up, w_down, ln3_g, ln3_b):
    n = 8
    f = lambda a: np.ascontiguousarray(np.asarray(a), dtype=np.float32)
    x = f(x); mem = f(mem)
    positions = np.ascontiguousarray(np.asarray(positions), dtype=np.int32)
    cf, cb, rel_idx = _host_constants()
    rb = f(rel_bias)[0]
    ii = np.arange(128)[:, None]
    jj = np.arange(256)[None, :]
    corner = np.broadcast_to((ii >= 64) & (jj < 64), (8, 128, 256))
    bt = np.ascontiguousarray(np.where(corner, np.float32(MASK_NEG), rb[:, rel_idx]), dtype=np.float32)
    c_h = np.broadcast_to(rb[:, 256][None, :], (128, 8))
    c_h2 = np.where(np.arange(128)[:, None] < 64, np.float32(MASK_NEG), c_h)
    btc = np.ascontiguousarray(np.concatenate([c_h, c_h2], axis=1), dtype=np.float32)
    lamb = np.ascontiguousarray(np.broadcast_to(f(diff_lambda)[0].reshape(1, 256), (128, 256)))
    subg = np.ascontiguousarray(f(subln_g)[0].reshape(128, 1))
    lnp = np.stack([np.stack([np.broadcast_to(f(g)[0][None, :], (128, D)),
                              np.broadcast_to(f(b)[0][None, :], (128, D))])
                    for g, b in ((ln1_g, ln1_b), (ln2_g, ln2_b), (ln3_g, ln3_b))])
    lnp = np.ascontiguousarray(lnp, dtype=np.float32)
    shared = {
        "w_in": f(w_in)[0], "w_o": f(w_o)[0], "w_mq": f(w_mq)[0], "w_mk": f(w_mk)[0],
        "w_mv": f(w_mv)[0], "w_mo": f(w_mo)[0], "w_up": f(w_up)[0], "w_down": f(w_down)[0],
        "lamb": lamb, "subg": subg, "bt": bt, "btc": btc, "lnp": lnp, "cf32": cf, "cbf": cb,
    }
    in_maps = []
    for b in range(n):
        m = dict(shared)
        m["x"] = x[b]
        m["mem"] = mem[b]
        m["posb"] = np.ascontiguousarray(np.broadcast_to(positions[b][None, :], (128, S_LEN)))
        in_maps.append(m)
    return in_maps


_CACHE = {}


def kernel(**inputs):
    in_maps = _prep(**inputs)
    if "nc" not in _CACHE:
        _CACHE["nc"] = build_program()
    res = run_bass_kernel_spmd(_CACHE["nc"], in_maps, core_ids=list(range(8)))
    return np.stack([r["out"] for r in res.results], axis=0).astype(np.float32)
```
